# Optimizing a Trainium2 kernel written in Bass

```python
import math
import numpy as np
import jax, jax.numpy as jnp
from jax import lax

D_MODEL = 1024
BATCH = 2
SEQ = 8192
DEPTH = 2

GRID_W = 64
MEM_LEN = 256
HEAD_DIM = 64
DA_HEADS = 4
DA_VDIM = 2 * HEAD_DIM
DA_QK = DA_HEADS * 2 * HEAD_DIM
DA_V = DA_HEADS * DA_VDIM
Q_BLOCK = 128
NA_HEADS = 8
NA_W = NA_HEADS * HEAD_DIM
NA_WIN_ROWS = 8
NA_WIN_COLS = 16
NA_QCOLS = 16
NA_KCOLS = 32
SC_W = 512
SC_K = 3
N_BRANCH = 3
BRANCH_W = 512
IN_SPLITS = (DA_QK, DA_QK, DA_V, NA_W, NA_W, NA_W, SC_W, SC_W, SC_W, N_BRANCH * D_MODEL)
IN_COLS = 7680
XA_HEADS = 4
XA_DIM = D_MODEL // XA_HEADS
D_FF = 2816
FFN_K = 3
ROPE_THETA = 10000.0
LN_EPS = 1e-5

kernel_name = 'hybrid_diff_natten_shortconv_encoder'


def layer_norm(x, g, b):
    xf = x.astype(jnp.float32)
    mu = jnp.mean(xf, axis=-1, keepdims=True)
    var = jnp.mean(jnp.square(xf - mu), axis=-1, keepdims=True)
    y = (xf - mu) * lax.rsqrt(var + LN_EPS)
    return (y * g.astype(jnp.float32) + b.astype(jnp.float32)).astype(x.dtype)


def rms_norm(x, g):
    xf = x.astype(jnp.float32)
    y = xf * lax.rsqrt(jnp.mean(jnp.square(xf), axis=-1, keepdims=True) + LN_EPS)
    return (y * g.astype(jnp.float32)).astype(x.dtype)


def rope(x, pos):
    d = x.shape[-1]
    half = d // 2
    inv = ROPE_THETA ** (-jnp.arange(half, dtype=jnp.float32) * 2.0 / d)
    ang = pos.astype(jnp.float32)[:, None] * inv[None, :]
    cos = jnp.cos(ang).astype(x.dtype)
    sin = jnp.sin(ang).astype(x.dtype)
    x1, x2 = x[..., :half], x[..., half:]
    return jnp.concatenate([x1 * cos - x2 * sin, x2 * cos + x1 * sin], axis=-1)


def dwconv(x, w):
    k, c = w.shape
    return lax.conv_general_dilated(x, w[:, None, :], window_strides=(1,), padding=[(k // 2, k // 2)],
                                    dimension_numbers=('NWC', 'WIO', 'NWC'), feature_group_count=c)


def diff_attention(q, k, v, lam, pos):
    b, s = q.shape[0], q.shape[1]
    q = rope(q.transpose(0, 2, 3, 1, 4), pos) * (HEAD_DIM ** -0.5)
    k = rope(k.transpose(0, 2, 3, 1, 4), pos)
    vt = v.transpose(0, 2, 1, 3)
    nb = s // Q_BLOCK
    qb = q.reshape(b, DA_HEADS, 2, nb, Q_BLOCK, HEAD_DIM).transpose(3, 0, 1, 2, 4, 5)

    def block(qblk):
        sc = jnp.einsum('bhmqd,bhmkd->bhmqk', qblk, k, preferred_element_type=jnp.float32)
        p = jax.nn.softmax(sc, axis=-1)
        w = p[:, :, 0] - lam * p[:, :, 1]
        return jnp.einsum('bhqk,bhkv->bhqv', w.astype(vt.dtype), vt)

    o = lax.map(block, qb)
    return o.transpose(1, 2, 0, 3, 4).reshape(b, DA_HEADS, s, DA_VDIM)


def _na_static(rows):
    wr = min(NA_WIN_ROWS, rows)
    n_cb = GRID_W // NA_QCOLS
    c = np.arange(GRID_W).reshape(n_cb, NA_QCOLS)
    cs = np.clip(c - NA_WIN_COLS // 2, 0, GRID_W - NA_WIN_COLS)
    kstart = np.minimum(cs[:, 0], GRID_W - NA_KCOLS)
    kcol = kstart[:, None] + np.arange(NA_KCOLS)
    rel = kcol[:, None, :] - c[:, :, None]
    valid = (kcol[:, None, :] >= cs[:, :, None]) & (kcol[:, None, :] < cs[:, :, None] + NA_WIN_COLS)
    dc_idx = np.clip(rel + NA_WIN_COLS - 1, 0, 2 * NA_WIN_COLS - 2)
    return wr, n_cb, kcol.astype(np.int32), dc_idx.astype(np.int32), valid


def neighbourhood_attention(q, k, v, rpb):
    b, s = q.shape[0], q.shape[1]
    rows = s // GRID_W
    wr, n_cb, kcol, dc_idx, valid = _na_static(rows)

    def grid(t):
        return t.reshape(b, rows, GRID_W, NA_HEADS, HEAD_DIM).transpose(0, 3, 1, 2, 4)

    qg = grid(q) * (HEAD_DIM ** -0.5)
    kg = grid(k)
    vg = grid(v)
    qr = qg.reshape(b, NA_HEADS, rows, n_cb, NA_QCOLS, HEAD_DIM).transpose(2, 0, 1, 3, 4, 5)
    kcol_j = jnp.asarray(kcol)
    mask = jnp.asarray(valid)[:, :, None, :]
    bias_c = rpb[:, :, jnp.asarray(dc_idx)]

    def row_block(args):
        r, qblk = args
        rs = jnp.clip(r - wr // 2, 0, rows - wr)
        k_rows = lax.dynamic_slice_in_dim(kg, rs, wr, axis=2)
        v_rows = lax.dynamic_slice_in_dim(vg, rs, wr, axis=2)
        k_blk = jnp.take(k_rows, kcol_j, axis=3)
        v_blk = jnp.take(v_rows, kcol_j, axis=3)
        sc = jnp.einsum('bhjqd,bhrjkd->bhjqrk', qblk, k_blk, preferred_element_type=jnp.float32)
        dr_idx = rs + jnp.arange(wr) - r + (NA_WIN_ROWS - 1)
        bias = jnp.take(bias_c, dr_idx, axis=1).transpose(0, 2, 3, 1, 4)
        sc = jnp.where(mask, sc + bias[None].astype(jnp.float32), -jnp.inf)
        p = jax.nn.softmax(sc.reshape(sc.shape[:4] + (wr * NA_KCOLS,)), axis=-1).reshape(sc.shape)
        return jnp.einsum('bhjqrk,bhrjkd->bhjqd', p.astype(v_blk.dtype), v_blk)

    o = lax.map(row_block, (jnp.arange(rows, dtype=jnp.int32), qr))
    return o.transpose(1, 0, 3, 4, 2, 5).reshape(b, s, NA_W)


def hybrid_mixer(x, w_in, lam_q1, lam_k1, lam_q2, lam_k2, subln_g, rpb, sc_conv_w, w_branch, w_mix_out, lam_init, pos):
    b, s, _ = x.shape
    proj = x @ w_in
    qa, ka, va, qb, kb, vb, gb, gc, hc, gates = jnp.split(proj, np.cumsum(IN_SPLITS)[:-1].tolist(), axis=-1)
    lam = (jnp.exp(jnp.sum(lam_q1.astype(jnp.float32) * lam_k1.astype(jnp.float32)))
           - jnp.exp(jnp.sum(lam_q2.astype(jnp.float32) * lam_k2.astype(jnp.float32))) + lam_init)
    oa = diff_attention(qa.reshape(b, s, DA_HEADS, 2, HEAD_DIM), ka.reshape(b, s, DA_HEADS, 2, HEAD_DIM),
                        va.reshape(b, s, DA_HEADS, DA_VDIM), lam, pos)
    ya = (rms_norm(oa, subln_g) * (1.0 - lam_init)).transpose(0, 2, 1, 3).reshape(b, s, DA_V)
    yb = neighbourhood_attention(qb, kb, vb, rpb)
    yc = gb * dwconv(gc * hc, sc_conv_w)
    g = jax.nn.sigmoid(gates.reshape(b, s, N_BRANCH, D_MODEL))
    merged = (g[:, :, 0] * (ya @ w_branch[0]) + g[:, :, 1] * (yb @ w_branch[1])
              + g[:, :, 2] * (yc @ w_branch[2]))
    return merged @ w_mix_out


def memory_cross_attention(x, mem, xa_q, xa_kv, xa_o):
    b, s, _ = x.shape
    q = (x @ xa_q).reshape(b, s, XA_HEADS, XA_DIM) * (XA_DIM ** -0.5)
    k, v = jnp.split(mem @ xa_kv, 2, axis=-1)
    k = k.reshape(b, mem.shape[1], XA_HEADS, XA_DIM)
    v = v.reshape(b, mem.shape[1], XA_HEADS, XA_DIM)
    p = jax.nn.softmax(jnp.einsum('bqhd,bkhd->bhqk', q, k, preferred_element_type=jnp.float32), axis=-1)
    o = jnp.einsum('bhqk,bkhd->bqhd', p.astype(v.dtype), v).reshape(b, s, D_MODEL)
    return o @ xa_o


def conv_ffn(x, ffn_w_in, ffn_conv_w, ffn_conv_b, ffn_w_out):
    u, gt = jnp.split(x @ ffn_w_in, 2, axis=-1)
    a = dwconv(gt, ffn_conv_w) + ffn_conv_b
    return (jax.nn.silu(a) * u) @ ffn_w_out


def setup_inputs(seed: int = 0) -> dict:
    key = jax.random.key(seed)
    ks = jax.random.split(key, 24)
    beta = (8.0 * DEPTH) ** -0.25

    def nrm(k, shape, scale):
        return jax.random.normal(k, shape, jnp.float32) * scale

    is_value = (False, False, True, False, False, True, False, False, True, False)
    col_scale = np.concatenate([np.full(n, beta if iv else 1.0, np.float32) for n, iv in zip(IN_SPLITS, is_value)])
    kv_scale = np.concatenate([np.ones(D_MODEL, np.float32), np.full(D_MODEL, beta, np.float32)])
    return {
        'x': nrm(ks[0], (BATCH, SEQ, D_MODEL), 1.0),
        'mem': nrm(ks[1], (BATCH, MEM_LEN, D_MODEL), 1.0),
        'emb_ln_g': 1.0 + nrm(ks[2], (D_MODEL,), 0.02),
        'emb_ln_b': nrm(ks[3], (D_MODEL,), 0.02),
        'w_in': nrm(ks[4], (DEPTH, D_MODEL, IN_COLS), D_MODEL ** -0.5) * jnp.asarray(col_scale),
        'lam_q1': nrm(ks[5], (DEPTH, HEAD_DIM), 0.1),
        'lam_k1': nrm(ks[6], (DEPTH, HEAD_DIM), 0.1),
        'lam_q2': nrm(ks[7], (DEPTH, HEAD_DIM), 0.1),
        'lam_k2': nrm(ks[8], (DEPTH, HEAD_DIM), 0.1),
        'subln_g': 1.0 + nrm(ks[9], (DEPTH, DA_VDIM), 0.02),
        'rpb': nrm(ks[10], (DEPTH, NA_HEADS, 2 * NA_WIN_ROWS - 1, 2 * NA_WIN_COLS - 1), 0.1),
        'sc_conv_w': nrm(ks[11], (DEPTH, SC_K, SC_W), SC_K ** -0.5),
        'w_branch': nrm(ks[12], (DEPTH, N_BRANCH, BRANCH_W, D_MODEL), BRANCH_W ** -0.5 * beta),
        'w_mix_out': nrm(ks[13], (DEPTH, D_MODEL, D_MODEL), D_MODEL ** -0.5 * beta),
        'xa_q': nrm(ks[14], (DEPTH, D_MODEL, D_MODEL), D_MODEL ** -0.5),
        'xa_kv': nrm(ks[15], (DEPTH, D_MODEL, 2 * D_MODEL), D_MODEL ** -0.5) * jnp.asarray(kv_scale),
        'xa_o': nrm(ks[16], (DEPTH, D_MODEL, D_MODEL), D_MODEL ** -0.5 * beta),
        'ffn_w_in': nrm(ks[17], (DEPTH, D_MODEL, 2 * D_FF), D_MODEL ** -0.5 * beta),
        'ffn_conv_w': nrm(ks[18], (DEPTH, FFN_K, D_FF), FFN_K ** -0.5),
        'ffn_conv_b': nrm(ks[19], (DEPTH, D_FF), 0.02),
        'ffn_w_out': nrm(ks[20], (DEPTH, D_FF, D_MODEL), D_FF ** -0.5 * beta),
        'ln_g': 1.0 + nrm(ks[21], (DEPTH, 3, D_MODEL), 0.02),
        'ln_b': nrm(ks[22], (DEPTH, 3, D_MODEL), 0.02),
    }


def reference(x, mem, emb_ln_g, emb_ln_b, w_in, lam_q1, lam_k1, lam_q2, lam_k2, subln_g, rpb, sc_conv_w,
              w_branch, w_mix_out, xa_q, xa_kv, xa_o, ffn_w_in, ffn_conv_w, ffn_conv_b, ffn_w_out, ln_g, ln_b):
    alpha = (2.0 * DEPTH) ** 0.25
    pos = jnp.arange(x.shape[1], dtype=jnp.int32)
    x = layer_norm(x, emb_ln_g, emb_ln_b)
    for l in range(DEPTH):
        lam_init = 0.8 - 0.6 * math.exp(-0.3 * l)
        h = hybrid_mixer(x, w_in[l], lam_q1[l], lam_k1[l], lam_q2[l], lam_k2[l], subln_g[l], rpb[l],
                         sc_conv_w[l], w_branch[l], w_mix_out[l], lam_init, pos)
        x = layer_norm(alpha * x + h, ln_g[l, 0], ln_b[l, 0])
        h = memory_cross_attention(x, mem, xa_q[l], xa_kv[l], xa_o[l])
        x = layer_norm(alpha * x + h, ln_g[l, 1], ln_b[l, 1])
        h = conv_ffn(x, ffn_w_in[l], ffn_conv_w[l], ffn_conv_b[l], ffn_w_out[l])
        x = layer_norm(alpha * x + h, ln_g[l, 2], ln_b[l, 2])
    return x
```

```python
import contextlib
import math
import numpy as np
import ml_dtypes
import concourse.bass as bass
import concourse.mybir as mybir
from concourse.bass_utils import run_bass_kernel_spmd

F32 = mybir.dt.float32
BF16 = mybir.dt.bfloat16
ALU = mybir.AluOpType
AF = mybir.ActivationFunctionType
NPBF = ml_dtypes.bfloat16

D = 1024
S = 8192
T = 2048
NT = 16
DEPTH = 2
DFF = 2816
NF = 22
EPS = 1e-5
ALPHA = (2.0 * DEPTH) ** 0.25
NEG = -30000.0

SEM_CHUNK = 1000
N_DMA_SEMS = 8
N_CC_SEMS = 16


class Buf:
    __slots__ = ("w", "rs", "rd")

    def __init__(self):
        self.w = None
        self.rs = {}
        self.rd = []


class Op:
    __slots__ = ("eng", "fn", "deps", "idx", "is_dma", "sig", "sem", "val", "prev", "cc")

    def __init__(self, eng, fn, is_dma):
        self.eng = eng
        self.fn = fn
        self.is_dma = is_dma
        self.deps = []
        self.sig = is_dma
        self.sem = None
        self.val = 0
        self.prev = None
        self.idx = 0
        self.cc = False


class _Dummy:
    def then_inc(self, *a, **k):
        return self


class Prog:
    ENGS = ("pe", "act", "dve", "pool", "sp")

    def __init__(self, nc):
        self.nc = nc
        self.ops = {e: [] for e in self.ENGS}
        self.stack = contextlib.ExitStack()
        self.n = 0
        self.prefix = ""
        self.pstack = None
        self.epoch = 0
        self.bar_deps = []
        self.bars = []
        self.eng_epoch = {e: 0 for e in self.ENGS}
        self.since = []

    def sbuf(self, name, shape, dtype):
        st = self.pstack if self.pstack is not None else self.stack
        return st.enter_context(self.nc.sbuf_tensor("sb_" + self.prefix + name, list(shape), dtype))

    def barrier(self):
        deps = list(self.since)
        for e in self.ENGS:
            for op in reversed(self.ops[e]):
                if not op.is_dma:
                    deps.append(op)
                    break
        for d in deps:
            d.sig = True
        self.epoch += 1
        self.bars.append((self.epoch, deps))
        self.since = []

    def psum(self, name, shape, dtype=F32):
        return self.stack.enter_context(self.nc.psum_tensor("pp_" + name, list(shape), dtype))

    def tile(self, name, shape, dtype):
        return (self.sbuf(name, shape, dtype), Buf())

    def ring(self, name, n, shape, dtype):
        return Ring([self.tile(f"{name}{i}", shape, dtype) for i in range(n)])

    def add(self, eng, fn, reads=(), writes=(), dma=False):
        op = Op(eng, fn, dma)
        deps = {}

        def need(d):
            if d is None or d is op:
                return
            if (not d.is_dma) and d.eng == eng and eng == "pe":
                return
            deps[id(d)] = d

        for b in reads:
            need(b.w)
        for b in writes:
            need(b.w)
            for d in b.rs.values():
                need(d)
            for d in b.rd:
                need(d)
        best = {}
        out = []
        for d in deps.values():
            if d.is_dma:
                out.append(d)
            else:
                k = d.eng
                if k not in best or best[k].idx < d.idx:
                    best[k] = d
        out.extend(best.values())
        if self.eng_epoch[eng] < self.epoch:
            have = {id(d) for d in out}
            for (ep, bdeps) in self.bars:
                if ep > self.eng_epoch[eng]:
                    for d in bdeps:
                        if id(d) not in have and d is not op and not ((not d.is_dma) and d.eng == eng and eng == "pe"):
                            have.add(id(d))
                            out.append(d)
            self.eng_epoch[eng] = self.epoch
        if dma:
            self.since.append(op)
        op.deps = out
        for d in out:
            d.sig = True
        op.idx = len(self.ops[eng])
        self.ops[eng].append(op)
        for b in reads:
            if dma:
                b.rd.append(op)
            else:
                b.rs[eng] = op
        for b in writes:
            b.w = op
            b.rs = {}
            b.rd = []
        return op

    def finish(self, out_bufs):
        op = Op("sp", lambda eng: _Dummy(), False)
        op.deps = [b.w for b in out_bufs if b.w is not None]
        op.idx = len(self.ops["sp"])
        self.ops["sp"].append(op)

    def emit(self):
        nc = self.nc
        st = self.stack
        for e in self.ENGS:
            k = 0
            j = 0
            esems = []
            dsems = []
            lastd = {}
            csems = []
            lastc = {}
            jc = 0
            for op in self.ops[e]:
                if op.cc:
                    si = jc % N_CC_SEMS
                    if si >= len(csems):
                        csems.append(st.enter_context(nc.semaphore(f"c_{e}_{si}")))
                    op.sem = csems[si]
                    op.val = jc // N_CC_SEMS + 1
                    op.prev = lastc.get(si)
                    lastc[si] = op
                    jc += 1
                elif op.is_dma:
                    si = j % N_DMA_SEMS
                    if si >= len(dsems):
                        dsems.append(st.enter_context(nc.semaphore(f"d_{e}_{si}")))
                    op.sem = dsems[si]
                    op.val = 16 * (j // N_DMA_SEMS + 1)
                    op.prev = lastd.get(si)
                    lastd[si] = op
                    j += 1
                elif op.sig:
                    ci = k // SEM_CHUNK
                    if ci >= len(esems):
                        esems.append(st.enter_context(nc.semaphore(f"s_{e}_{ci}")))
                    op.sem = esems[ci]
                    op.val = k % SEM_CHUNK + 1
                    k += 1
        block = st.enter_context(nc.Block())
        prog = self

        def run(e, eng):
            waited = {}
            for op in prog.ops[e]:
                needs = {}
                dl = list(op.deps)
                if op.prev is not None:
                    dl.append(op.prev)
                for d in dl:
                    key = id(d.sem)
                    if key not in needs or needs[key][1] < d.val:
                        needs[key] = (d.sem, d.val)
                for key, (sem, val) in needs.items():
                    if waited.get(key, 0) < val:
                        eng.wait_ge(sem, val)
                        waited[key] = val
                ins = op.fn(eng)
                if op.sig:
                    ins.then_inc(op.sem, 16 if (op.is_dma and not op.cc) else 1)

        if self.ops["pe"]:
            @block.tensor
            def _(eng):
                run("pe", eng)
        if self.ops["act"]:
            @block.scalar
            def _(eng):
                run("act", eng)
        if self.ops["dve"]:
            @block.vector
            def _(eng):
                run("dve", eng)
        if self.ops["pool"]:
            @block.gpsimd
            def _(eng):
                run("pool", eng)
        if self.ops["sp"]:
            @block.sync
            def _(eng):
                run("sp", eng)

    def close(self):
        self.stack.close()


class Ring:
    def __init__(self, items):
        self.items = items
        self.i = 0

    def next(self):
        it = self.items[self.i % len(self.items)]
        self.i += 1
        return it


class Ctx:
    def __init__(self, nc):
        self.nc = nc
        self.P = Prog(nc)
        P = self.P
        self.ps = Ring([(P.psum(f"ps{i}", [128, 512]), Buf()) for i in range(8)])
        self.outs = []
        self.bind = {}
        self.suffix = ""
        self._rank = None
        self.last_stores = []

    def dram_in(self, name, shape, dtype):
        if name in self.bind:
            return self.bind[name]
        return self.nc.dram_tensor(name + self.suffix, list(shape), dtype, kind="ExternalInput").ap()

    def dram_out(self, name, shape, dtype):
        if name in self.bind:
            return self.bind[name]
        return self.nc.dram_tensor(name + self.suffix, list(shape), dtype, kind="ExternalOutput").ap()

    def scratch(self, name, shape, dtype):
        return self.nc.dram_tensor(name, list(shape), dtype, kind="Internal").ap()

    def rank(self, e):
        if self._rank is None:
            self._rank = e.partition_id() % 4
        return self._rank

    def load_dyn(self, dst_ap, dst_buf, src_fn, q="sp", reads=()):
        self.P.add(q, lambda e: e.dma_start(out=dst_ap, in_=src_fn(e)), reads=list(reads), writes=[dst_buf], dma=True)

    def allgather(self, src, dst, writes=(), after=()):
        groups = [[0, 1, 2, 3], [4, 5, 6, 7]]
        op = self.P.add("pool", lambda e: e.collective_compute("AllGather", ALU.bypass, replica_groups=groups, ins=[src], outs=[dst]),
                        writes=list(writes), dma=True)
        have = {id(d) for d in op.deps}
        for d in after:
            if id(d) not in have:
                op.deps.append(d)
                d.sig = True
        op.cc = True
        return op

    @contextlib.contextmanager
    def phase(self, prefix):
        self.P.prefix = prefix
        self.P.pstack = contextlib.ExitStack()
        try:
            yield
        finally:
            self.P.barrier()
            self.P.pstack.close()
            self.P.pstack = None

    def load(self, dst_ap, dst_buf, src_ap, q="sp", reads=()):
        self.P.add(q, lambda e: e.dma_start(out=dst_ap, in_=src_ap), reads=list(reads), writes=[dst_buf], dma=True)

    def store(self, dst_ap, src_ap, src_buf, q="sp", slow=False):
        b = Buf()
        if slow:
            op = self.P.add(q, lambda e: e.dma_start(out=dst_ap, in_=src_ap, allow_slow_non_contiguous=True), reads=[src_buf], writes=[b], dma=True)
        else:
            op = self.P.add(q, lambda e: e.dma_start(out=dst_ap, in_=src_ap), reads=[src_buf], writes=[b], dma=True)
        self.last_stores.append(op)
        self.outs.append(b)
        return op

    def consts(self):
        P = self.P
        self.ident, self.identb = P.tile("ident", [128, 128], BF16)
        ident = self.ident
        P.add("pool", lambda e: e.memset(ident[:], 1.0), writes=[self.identb])
        P.add("pool", lambda e: e.affine_select(out=ident[:], in_=ident[:], pattern=[[-1, 128]],
                                                compare_op=ALU.is_equal, fill=0.0, base=0, channel_multiplier=1),
              reads=[self.identb], writes=[self.identb])
        self.ones, self.onesb = P.tile("ones", [128, 128], BF16)
        ones = self.ones
        P.add("pool", lambda e: e.memset(ones[:], 1.0), writes=[self.onesb])

    def mm(self, ps_ap, psb, pairs, reads):
        pairs = list(pairs)

        def fn(e):
            n = len(pairs)
            ins = None
            for i, (l, r) in enumerate(pairs):
                ins = e.matmul(ps_ap, lhsT=l, rhs=r, start=(i == 0), stop=(i == n - 1))
            return ins

        self.P.add("pe", fn, reads=reads, writes=[psb])

    def done(self):
        self.P.finish(self.outs)
        self.P.emit()
        self.P.close()


def wview(w_ap):
    return w_ap.rearrange("(kc p) n -> p kc n", p=128)


def layer_norm_tile(cx, r, rb, g, gb_, b, bb_, out, outb, tmp):
    P = cx.P
    st, stb = tmp["st"].next()
    mv, mvb = tmp["mv"].next()
    P.add("dve", lambda e: e.bn_stats(out=st[:, 0:6], in_=r[:, 0:512]), reads=[rb], writes=[stb])
    P.add("dve", lambda e: e.bn_stats(out=st[:, 6:12], in_=r[:, 512:1024]), reads=[rb, stb], writes=[stb])
    P.add("dve", lambda e: e.bn_aggr(out=mv[:, 0:2], in_=st[:, 0:12]), reads=[stb], writes=[mvb])
    P.add("act", lambda e: e.activation(out=mv[:, 2:3], in_=mv[:, 1:2], func=AF.Sqrt, bias=EPS, scale=1.0),
          reads=[mvb], writes=[mvb])
    P.add("dve", lambda e: e.reciprocal(out=mv[:, 3:4], in_=mv[:, 2:3]), reads=[mvb], writes=[mvb])
    P.add("dve", lambda e: e.tensor_scalar(out=out[:], in0=r[:], scalar1=mv[:, 0:1], scalar2=mv[:, 3:4],
                                           op0=ALU.subtract, op1=ALU.mult), reads=[rb, mvb], writes=[outb])
    P.add("pool", lambda e: e.tensor_tensor(out=out[:], in0=out[:], in1=g[:], op=ALU.mult),
          reads=[outb, gb_], writes=[outb])
    P.add("pool", lambda e: e.tensor_tensor(out=out[:], in0=out[:], in1=b[:], op=ALU.add),
          reads=[outb, bb_], writes=[outb])


def ln_tmp(P, tag=""):
    return {"st": P.ring("lnst" + tag, 3, [128, 12], F32), "mv": P.ring("lnmv" + tag, 3, [128, 4], F32)}


def bcast_row(ap_1d, n):
    return ap_1d.rearrange("(o n) -> o n", o=1).broadcast(0, 128) if hasattr(ap_1d, "broadcast") else None


def transpose_tile(cx, xb16, xb16b, xT, xTb, tt):
    for half in range(2):
        ps, psb = cx.ps.next()
        pairs = []

        def fn(e, ps=ps, half=half):
            ins = None
            for j in range(4):
                d = half * 4 + j
                ins = e.matmul(ps[:, j * 128:(j + 1) * 128], lhsT=xb16[:, d * 128:(d + 1) * 128], rhs=cx.ident[:],
                               start=True, stop=True)
            return ins

        cx.P.add("pe", fn, reads=[xb16b, cx.identb], writes=[psb])
        cx.P.add("act", lambda e, ps=ps, half=half: e.copy(
            out=xT[:, half * 4:half * 4 + 4, tt * 128:(tt + 1) * 128],
            in_=ps[:, 0:512].rearrange("p (j t) -> p j t", j=4)), reads=[psb], writes=[xTb])


P1_BLOCKS = ["qa", "qap", "ka", "kap", "qb", "kb", "gb", "gc", "hc", "va", "vb"]


def build_p1(cx, first):
    P = cx.P
    x_d = cx.dram_in("x", [T, D], F32)
    w_d = cx.dram_in("w", [D, 11 * 512], F32)
    cos_d = cx.dram_in("cosT", [128, T], F32)
    sin_d = cx.dram_in("sinT", [128, T], F32)
    if first:
        g_d = cx.dram_in("g", [128, D], F32)
        b_d = cx.dram_in("b", [128, D], F32)
        xln_d = cx.dram_out("xln", [T, D], F32)
    xT_d = cx.dram_out("xT", [D, T], BF16)
    qaT_d = cx.dram_out("qaT", [512, T], BF16)
    kaT_d = cx.dram_out("kaT", [512, T], BF16)
    qbT_d = cx.dram_out("qbT", [512, T], BF16)
    kbT_d = cx.dram_out("kbT", [512, T], BF16)
    va_d = cx.dram_out("va", [T, 512], BF16)
    vb_d = cx.dram_out("vb", [T, 512], BF16)
    gbT_d = cx.dram_out("gbT", [512, T], F32)
    pT_d = cx.dram_out("pT", [512, T], F32)

    xT, xTb = P.tile("xT", [128, 8, T], BF16)
    cosT, cosb = P.tile("cosT_s", [128, T], F32)
    sinT, sinb = P.tile("sinT_s", [128, T], F32)
    cx.load(cosT[:], cosb, cos_d)
    cx.load(sinT[:], sinb, sin_d)
    if first:
        g, gb_ = P.tile("g_s", [128, D], F32)
        b, bb_ = P.tile("b_s", [128, D], F32)
        cx.load(g[:], gb_, g_d)
        cx.load(b[:], bb_, b_d)
        tmp = ln_tmp(P)
    xin = P.ring("xin", 2, [128, D], F32)
    xo = P.ring("xo", 2, [128, D], F32)
    xbf = P.ring("xbf", 2, [128, D], BF16)
    for tt in range(NT):
        xt, xtb = xin.next()
        cx.load(xt[:], xtb, x_d[tt * 128:(tt + 1) * 128, :])
        if first:
            o, ob = xo.next()
            layer_norm_tile(cx, xt, xtb, g, gb_, b, bb_, o, ob, tmp)
            cx.store(xln_d[tt * 128:(tt + 1) * 128, :], o[:], ob)
        else:
            o, ob = xt, xtb
        xb16, xb16b = xbf.next()
        P.add("dve", lambda e, o=o, xb16=xb16: e.tensor_copy(out=xb16[:], in_=o[:]), reads=[ob], writes=[xb16b])
        transpose_tile(cx, xb16, xb16b, xT, xTb, tt)
    cx.store(xT_d.rearrange("(dc p) t -> p dc t", p=128), xT[:], xTb)

    wr = P.ring("wblk", 3, [128, 8, 512], BF16)
    wv = wview(w_d)

    def load_w(i):
        wt, wtb = wr.next()
        cx.load(wt[:], wtb, wv[:, :, i * 512:(i + 1) * 512], q="pool")
        return wt, wtb

    st16 = P.ring("st16", 3, [128, T], BF16)
    st32 = P.ring("st32", 3, [128, T], F32)
    t1r = P.ring("t1", 2, [128, 512], F32)
    t2r = P.ring("t2", 2, [128, 512], F32)

    def fm_group(wt, wtb, c4, tq):
        ps, psb = cx.ps.next()
        cx.mm(ps[:], psb, [(wt[:, d, c4 * 128:(c4 + 1) * 128], xT[:, d, tq * 512:(tq + 1) * 512]) for d in range(8)],
              [wtb, xTb])
        return ps, psb

    cx.last_stores = []
    for (i0, out_d) in ((0, qaT_d), (2, kaT_d)):
        wa, wab = load_w(i0)
        wp, wpb = load_w(i0 + 1)
        for c4 in range(4):
            sg, sgb = st16.next()
            for tq in range(4):
                pa, pab = fm_group(wa, wab, c4, tq)
                pp, ppb = fm_group(wp, wpb, c4, tq)
                t1, t1b = t1r.next()
                t2, t2b = t2r.next()
                sl = slice(tq * 512, (tq + 1) * 512)
                P.add("dve", lambda e, t1=t1, pa=pa, sl=sl: e.tensor_tensor(out=t1[:], in0=pa[:], in1=cosT[:, sl], op=ALU.mult),
                      reads=[pab, cosb], writes=[t1b])
                P.add("dve", lambda e, t2=t2, pp=pp, sl=sl: e.tensor_tensor(out=t2[:], in0=pp[:], in1=sinT[:, sl], op=ALU.mult),
                      reads=[ppb, sinb], writes=[t2b])
                P.add("pool", lambda e, t1=t1, t2=t2, sg=sg, sl=sl: e.tensor_tensor(out=sg[:, sl], in0=t1[:], in1=t2[:], op=ALU.add),
                      reads=[t1b, t2b], writes=[sgb])
            cx.store(out_d[c4 * 128:(c4 + 1) * 128, :], sg[:], sgb)
    cx.last_stores_A = list(cx.last_stores)
    vst = P.ring("vst", 3, [128, 512], BF16)

    def tm_block(i0, out_d):
        wa, wab = load_w(i0)
        for tt in range(NT):
            ps, psb = cx.ps.next()
            cx.mm(ps[:], psb, [(xT[:, d, tt * 128:(tt + 1) * 128], wa[:, d, :]) for d in range(8)], [wab, xTb])
            v, vb_ = vst.next()
            P.add("act", lambda e, ps=ps, v=v: e.copy(out=v[:], in_=ps[:]), reads=[psb], writes=[vb_])
            cx.store(out_d(tt), v[:].rearrange("p (j c) -> p j c", j=4), vb_)

    cx.last_stores = []
    tm_block(9, va_d)
    if "setA_hook" in cx.bind:
        cx.bind["setA_hook"](cx.last_stores_A + cx.last_stores)
    for (i0, out_d) in ((4, qbT_d), (5, kbT_d)):
        wa, wab = load_w(i0)
        for c4 in range(4):
            sg, sgb = st16.next()
            for tq in range(4):
                pa, pab = fm_group(wa, wab, c4, tq)
                sl = slice(tq * 512, (tq + 1) * 512)
                P.add("act", lambda e, pa=pa, sg=sg, sl=sl: e.copy(out=sg[:, sl], in_=pa[:]), reads=[pab], writes=[sgb])
            cx.store(out_d[c4 * 128:(c4 + 1) * 128, :], sg[:], sgb)
    wa, wab = load_w(6)
    for c4 in range(4):
        sg, sgb = st32.next()
        for tq in range(4):
            pa, pab = fm_group(wa, wab, c4, tq)
            sl = slice(tq * 512, (tq + 1) * 512)
            P.add("act", lambda e, pa=pa, sg=sg, sl=sl: e.copy(out=sg[:, sl], in_=pa[:]), reads=[pab], writes=[sgb])
        cx.store(gbT_d[c4 * 128:(c4 + 1) * 128, :], sg[:], sgb)
    e1s, e1sb = P.tile("e1s", [128, 4, 2], F32)
    wa, wab = load_w(7)
    wp, wpb = load_w(8)
    for c4 in range(4):
        sg, sgb = st32.next()
        for tq in range(4):
            pa, pab = fm_group(wa, wab, c4, tq)
            pp, ppb = fm_group(wp, wpb, c4, tq)
            t1, t1b = t1r.next()
            sl = slice(tq * 512, (tq + 1) * 512)
            P.add("act", lambda e, pa=pa, t1=t1: e.copy(out=t1[:], in_=pa[:]), reads=[pab], writes=[t1b])
            P.add("dve", lambda e, pp=pp, t1=t1, sg=sg, sl=sl: e.tensor_tensor(out=sg[:, sl], in0=pp[:], in1=t1[:], op=ALU.mult),
                  reads=[ppb, t1b], writes=[sgb])
        cx.store(pT_d[c4 * 128:(c4 + 1) * 128, :], sg[:], sgb)
        P.add("act", lambda e, sg=sg, c4=c4: e.copy(out=e1s[:, c4, 0:1], in_=sg[:, 0:1]), reads=[sgb], writes=[e1sb])
        P.add("act", lambda e, sg=sg, c4=c4: e.copy(out=e1s[:, c4, 1:2], in_=sg[:, T - 1:T]), reads=[sgb, e1sb], writes=[e1sb])
    cx.store(cx.bind["E1src"], e1s[:].rearrange("p a b -> p (a b)"), e1sb)
    tm_block(10, vb_d)


def _perm64(n):
    idx = np.arange(n).reshape(-1, 64)
    return np.concatenate([idx[:, 32:], idx[:, :32]], axis=1).reshape(-1)


def rope_tables(tb):
    half = 32
    inv = (np.float32(10000.0) ** (-np.arange(half, dtype=np.float32) * np.float32(2.0) / np.float32(64))).astype(np.float32)
    pos = np.arange(tb * T, (tb + 1) * T, dtype=np.float32)
    ang = (pos[None, :] * inv[:, None]).astype(np.float32)
    c = np.cos(ang).astype(np.float32)
    s = np.sin(ang).astype(np.float32)
    cosT = np.concatenate([c, c, c, c], axis=0)
    sinT = np.concatenate([-s, s, -s, s], axis=0)
    return np.ascontiguousarray(cosT), np.ascontiguousarray(sinT)


def p1_weight(w):
    sl = lambda i: w[:, i * 512:(i + 1) * 512]
    qa, ka, va, qb, kb, vb, gb, gc, hc = [sl(i) for i in range(9)]
    pm = _perm64(512)
    return np.ascontiguousarray(np.concatenate([qa, qa[:, pm], ka, ka[:, pm], qb, kb, gb, gc, hc, va, vb], axis=1))


def p1_inputs(inp, l, xfull, first):
    w = p1_weight(np.asarray(inp["w_in"][l]))
    maps = []
    for c in range(8):
        b, tb = c // 4, c % 4
        cosT, sinT = rope_tables(tb)
        m = {"x": np.ascontiguousarray(xfull[b, tb * T:(tb + 1) * T]), "w": w, "cosT": cosT, "sinT": sinT}
        if first:
            m["g"] = np.ascontiguousarray(np.broadcast_to(np.asarray(inp["emb_ln_g"])[None, :], (128, D)))
            m["b"] = np.ascontiguousarray(np.broadcast_to(np.asarray(inp["emb_ln_b"])[None, :], (128, D)))
        maps.append(m)
    return maps


def build_p2(cx, lam_init, n_qb=S // 512, n_rows=128):
    P = cx.P
    tab_d = cx.dram_in("tab", [128, 2 * 14 * 64], F32)
    lamv_d = cx.dram_in("lamv", [128, 4 * 64], F32)
    gs_d = cx.dram_in("gs", [128, 1], F32)
    g2s = cx.bind["G2src"].rearrange("(t w p) c -> t w p c", t=4, w=2)
    banks = cx.ps.items
    s_ring = Ring(banks[0:3])
    o_ring = Ring(banks[3:5])
    z_ring = Ring(banks[5:7])
    m_bank = banks[7]

    qT, qTb = P.tile("qT", [128, S], BF16)
    kT, kTb = P.tile("kT", [128, S], BF16)
    v, vb_ = P.tile("v", [128, 64, 128], BF16)
    nqT, nqTb = P.tile("nqT", [128, S], BF16)
    nkT, nkTb = P.tile("nkT", [128, S], BF16)
    nve, nveb = P.tile("nve", [128, 64, 128], BF16)
    nvo, nvob = P.tile("nvo", [128, 63, 128], BF16)
    tab, tabb = P.tile("tab", [128, 2, 14, 64], F32)
    m1 = cx.bind["M1"].rearrange("(w t p) c -> t w p c", t=4, w=6)
    rA = [cx.bind["m1A"]]
    for t in range(4):
        ts_ = slice(t * T, (t + 1) * T)
        cx.load(qT[:, ts_], qTb, m1[t, 0], reads=rA)
        cx.load(kT[:, ts_], kTb, m1[t, 1], reads=rA)
        cx.load(v[:, t * 16:(t + 1) * 16, :].rearrange("p a b -> p (a b)"), vb_, m1[t, 2], reads=rA)

    def na_loads():
        cx.bind["m1b_hook"]()
        rB = [cx.bind["m1B"]]
        for t in range(4):
            ts_ = slice(t * T, (t + 1) * T)
            cx.load(nqT[:, ts_], nqTb, m1[t, 3], reads=rB)
            cx.load(nkT[:, ts_], nkTb, m1[t, 4], reads=rB)
            cx.load(nve[:, t * 16:(t + 1) * 16, :].rearrange("p a b -> p (a b)"), nveb, m1[t, 5], reads=rB)
            n0 = 15 if t == 3 else 16
            cx.load(nvo[0:64, t * 16:t * 16 + n0, :].rearrange("p a b -> p (a b)"), nvob, m1[t, 5, 64:128, 0:n0 * 128], reads=rB)
            a0 = t * 16 - 1 if t > 0 else 0
            c0 = 0 if t > 0 else 128
            cx.load(nvo[64:128, a0:t * 16 + 15, :].rearrange("p a b -> p (a b)"), nvob, m1[t, 5, 0:64, c0:T], reads=rB)

    cx.load(tab[:], tabb, tab_d.rearrange("p (h s c) -> p h s c", h=2, s=14))
    lamv, lamvb = P.tile("lamv", [128, 4, 64], F32)
    gs, gsb = P.tile("gs", [128, 1], F32)
    cx.load(lamv[:], lamvb, lamv_d.rearrange("p (a b) -> p a b", b=64))
    cx.load(gs[:], gsb, gs_d)
    sc, scb = P.tile("sc", [128, 8], F32)
    lp, lpb = P.tile("lp", [128, 2, 64], F32)
    P.add("dve", lambda e: e.tensor_tensor(out=lp[:, 0, :], in0=lamv[:, 0, :], in1=lamv[:, 1, :], op=ALU.mult),
          reads=[lamvb], writes=[lpb])
    P.add("dve", lambda e: e.tensor_tensor(out=lp[:, 1, :], in0=lamv[:, 2, :], in1=lamv[:, 3, :], op=ALU.mult),
          reads=[lamvb, lpb], writes=[lpb])
    P.add("dve", lambda e: e.reduce_sum(out=sc[:, 0:1], in_=lp[:, 0, :], axis=mybir.AxisListType.X), reads=[lpb], writes=[scb])
    P.add("dve", lambda e: e.reduce_sum(out=sc[:, 1:2], in_=lp[:, 1, :], axis=mybir.AxisListType.X), reads=[lpb, scb], writes=[scb])
    P.add("act", lambda e: e.activation(out=sc[:, 2:4], in_=sc[:, 0:2], func=AF.Exp), reads=[scb], writes=[scb])
    P.add("dve", lambda e: e.tensor_tensor(out=sc[:, 4:5], in0=sc[:, 3:4], in1=sc[:, 2:3], op=ALU.subtract), reads=[scb], writes=[scb])
    P.add("dve", lambda e: e.tensor_scalar(out=sc[:, 4:5], in0=sc[:, 4:5], scalar1=-float(lam_init), scalar2=None, op0=ALU.add),
          reads=[scb], writes=[scb])
    P.add("dve", lambda e: e.tensor_scalar(out=sc[:, 5:6], in0=gs[:, 0:1], scalar1=float(1.0 - lam_init), scalar2=None, op0=ALU.mult),
          reads=[scb, gsb], writes=[scb])
    ones32, ones32b = P.tile("ones32", [128, 128], F32)
    P.add("dve", lambda e: e.memset(ones32[:], 1.0 / 128.0), writes=[ones32b])

    pT_ring = P.ring("pT", 6, [128, 512], BF16)
    rz_ring = P.ring("rz", 2, [128, 512], F32)
    om_ring = P.ring("om", 4, [128, 512], F32)
    oa_ring = P.ring("oa", 2, [128, 512], F32)
    sq_ring = P.ring("sq", 2, [128, 512], F32)
    ya_ring = P.ring("ya", 2, [128, 512], BF16)

    NKC = S // 128
    cx.last_stores = []
    ones32u, ones32ub = P.tile("ones32u", [128, 128], F32)
    P.add("dve", lambda e: e.memset(ones32u[:], 1.0), writes=[ones32ub])
    dq = []
    s4_ring = Ring(banks[0:4])
    O_banks = [banks[4], banks[5]]
    z_bank = banks[6]
    oc_ring = P.ring("ocp", 4, [128, 512], F32)
    za_ring2 = P.ring("zacc2", 4, [128, 512], F32)
    for qb in range(n_qb):
        qs = slice(qb * 512, (qb + 1) * 512)
        zas = [[za_ring2.next()] for m in range(2)]

        def qk(kc):
            out = []
            for m in range(2):
                rows = slice(m * 64, (m + 1) * 64)
                s_, sb_ = s4_ring.next()
                cx.mm(s_[:], sb_, [(kT[rows, kc * 128:(kc + 1) * 128], qT[rows, qs])], [kTb, qTb])
                out.append((s_, sb_))
            return out

        cur = qk(0)
        for kc in range(NKC):
            if dq and kc % 4 == 3:
                dq.pop(0)()
            nxt = qk(kc + 1) if kc + 1 < NKC else None
            pts = []
            for m in range(2):
                s_, sb_ = cur[m]
                pt, ptb = pT_ring.next()
                P.add("act", lambda e, s_=s_, pt=pt: e.activation(out=pt[:], in_=s_[:], func=AF.Exp, scale=0.125),
                      reads=[sb_], writes=[ptb])
                pts.append((pt, ptb))
            for m in range(2):
                pt, ptb = pts[m]
                O, Ob = O_banks[m]
                P.add("pe", lambda e, kc=kc, pt=pt, O=O: e.matmul(O[:], lhsT=v[:, kc, :], rhs=pt[:], start=(kc == 0), stop=(kc == NKC - 1)),
                      reads=[vb_, ptb], writes=[Ob])
                za, zab = zas[m][0]
                if kc < 1:
                    P.add("dve", lambda e, za=za, pt=pt: e.tensor_copy(out=za[:], in_=pt[:]), reads=[ptb], writes=[zab])
                else:
                    P.add("dve", lambda e, za=za, pt=pt: e.tensor_tensor(out=za[:], in0=za[:], in1=pt[:], op=ALU.add),
                          reads=[ptb, zab], writes=[zab])
            cur = nxt
        ocs = []
        for m in range(2):
            oc, ocb = oc_ring.next()
            O, Ob = O_banks[m]
            P.add("act", lambda e, oc=oc, O=O: e.copy(out=oc[:], in_=O[:]), reads=[Ob], writes=[ocb])
            ocs.append((oc, ocb))
        oms = [om_ring.next() for m in range(2)]

        def mk_stage0(m, zas=zas, ocs=ocs, oms=oms):
            def stage0():
                Z, Zb = z_bank
                cx.mm(Z[:], Zb, [(ones32u[:], z_[0][:]) for z_ in zas[m]], [ones32ub] + [z_[1] for z_ in zas[m]])
                rz, rzb = rz_ring.next()
                P.add("dve", lambda e: e.reciprocal(out=rz[:], in_=Z[:]), reads=[Zb], writes=[rzb])
                oc, ocb = ocs[m]
                om, omb = oms[m]
                P.add("dve", lambda e: e.tensor_tensor(out=om[:], in0=oc[:], in1=rz[:], op=ALU.mult),
                      reads=[ocb, rzb], writes=[omb])
            return stage0

        oa, oab = oa_ring.next()
        sq, sqb = sq_ring.next()

        def stage0b(oms=oms, oa=oa, oab=oab):
            (o0, o0b), (o1, o1b) = oms
            P.add("dve", lambda e: e.scalar_tensor_tensor(out=oa[:], in0=o1[:], scalar=sc[:, 4:5], in1=o0[:],
                                                          op0=ALU.mult, op1=ALU.add),
                  reads=[o0b, o1b, scb], writes=[oab])

        def stage1(oa=oa, oab=oab, sq=sq, sqb=sqb):
            P.add("act", lambda e: e.activation(out=sq[:], in_=oa[:], func=AF.Square), reads=[oab], writes=[sqb])
            ms, msb = m_bank
            cx.mm(ms[:], msb, [(ones32[:], sq[:])], [ones32b, sqb])

        def stage2(oa=oa, oab=oab, sq=sq, sqb=sqb, qb=qb):
            ms, msb = m_bank
            P.add("act", lambda e: e.activation(out=sq[:], in_=ms[:], func=AF.Sqrt, bias=EPS, scale=1.0),
                  reads=[msb], writes=[sqb])
            P.add("dve", lambda e: e.reciprocal(out=sq[:], in_=sq[:]), reads=[sqb], writes=[sqb])
            ya, yab = ya_ring.next()
            P.add("dve", lambda e: e.scalar_tensor_tensor(out=ya[:], in0=oa[:], scalar=sc[:, 5:6], in1=sq[:],
                                                          op0=ALU.mult, op1=ALU.mult),
                  reads=[oab, sqb, scb], writes=[yab])
            cx.store(g2s[qb // 4, 0, :, (qb % 4) * 512:(qb % 4 + 1) * 512], ya[:], yab)

        dq.extend([mk_stage0(0), mk_stage0(1), stage0b, stage1, stage2])
    while dq:
        dq.pop(0)()

    if "ya_hook" in cx.bind:
        cx.bind["ya_hook"](list(cx.last_stores))
    na_loads()
    ybT = [P.tile(f"ybT{hh}", [64, S], BF16) for hh in range(2)]
    t_ring = P.ring("nt", 3, [128, 256], F32)
    np_ring = P.ring("npT", 3, [128, 256], BF16)
    nrz_ring = P.ring("nrz", 3, [64, 64], F32)
    pz_ring = Ring(banks[3:7])
    s2_ring = Ring(banks[0:4])
    pz_ring = Ring(banks[4:8])
    nzs_ring = P.ring("nzs", 3, [64, 64], F32)

    def nqk(r):
        rs = min(max(r - 4, 0), 120)
        tiles = [s2_ring.next() for hh in range(2)]

        def fn(e):
            ins = None
            for jj in range(4):
                k0 = (rs + 2 * jj) * 64
                for hh in range(2):
                    rows = slice(hh * 64, (hh + 1) * 64)
                    ins = e.matmul(tiles[hh][0][:, jj * 64:(jj + 1) * 64], lhsT=nkT[rows, k0:k0 + 128], rhs=nqT[rows, r * 64:(r + 1) * 64],
                                   start=True, stop=True)
            return ins

        P.add("pe", fn, reads=[nkTb, nqTb], writes=[tiles[0][1], tiles[1][1]])
        return tiles

    cur = nqk(0) if n_rows else None
    for r in range(n_rows):
        nxt = nqk(r + 1) if r + 1 < n_rows else None
        rs = min(max(r - 4, 0), 120)
        vv = rs - r + 7
        for hh in range(2):
            s, sb = cur[hh]
            t, tb_ = t_ring.next()
            P.add("dve", lambda e, t=t, s=s, hh=hh, vv=vv: e.scalar_tensor_tensor(
                out=t[:].rearrange("p (a b) -> p a b", b=64), in0=s[:, 0:256].rearrange("p (a b) -> p a b", b=64), scalar=0.125,
                in1=tab[:, hh, vv:vv + 7:2, :], op0=ALU.mult, op1=ALU.add), reads=[sb, tabb], writes=[tb_])
            pt, ptb = np_ring.next()
            P.add("act", lambda e, t=t, pt=pt: e.activation(out=pt[:], in_=t[:], func=AF.Exp), reads=[tb_], writes=[ptb])
            pz, pzb = pz_ring.next()

            def fn(e, rs=rs, hh=hh, pt=pt, pz=pz):
                for jj in range(4):
                    kr = rs + 2 * jj
                    vt = nve[:, kr // 2, hh * 64:(hh + 1) * 64] if kr % 2 == 0 else nvo[:, (kr - 1) // 2, hh * 64:(hh + 1) * 64]
                    e.matmul(pz[0:64, 0:64], lhsT=vt, rhs=pt[:, jj * 64:(jj + 1) * 64], start=(jj == 0), stop=(jj == 3))
                return e.matmul(pz[0:64, 64:320], lhsT=cx.ones[:, 0:64], rhs=pt[:, 0:256], start=True, stop=True)

            P.add("pe", fn, reads=[nveb, nvob, ptb, cx.onesb], writes=[pzb])
            zs, zsb = nzs_ring.next()
            P.add("dve", lambda e, zs=zs, pz=pz: e.reduce_sum(out=zs[:], in_=pz[0:64, 64:320].rearrange("p (a b) -> p b a", b=64),
                                                              axis=mybir.AxisListType.X), reads=[pzb], writes=[zsb])
            rz, rzb = nrz_ring.next()
            P.add("dve", lambda e, rz=rz, zs=zs: e.reciprocal(out=rz[:], in_=zs[:]), reads=[zsb], writes=[rzb])
            yt, ytb = ybT[hh]
            P.add("dve", lambda e, yt=yt, pz=pz, rz=rz, r=r: e.tensor_tensor(out=yt[:, r * 64:(r + 1) * 64], in0=pz[0:64, 0:64], in1=rz[:], op=ALU.mult),
                  reads=[pzb, rzb], writes=[ytb])
        cur = nxt
    for hh in range(2):
        yt, ytb = ybT[hh]
        for tb_ in range(4):
            cx.store(g2s[tb_, 1, hh * 64:(hh + 1) * 64, :], yt[:, tb_ * T:(tb_ + 1) * T], ytb)


def na_table(rpb_l, heads):
    p = np.arange(128)
    half = p // 64
    kc = p % 64
    c = np.arange(64)
    cs = np.clip(c - 8, 0, 48)
    valid = (kc[:, None] >= cs[None, :]) & (kc[:, None] < cs[None, :] + 16)
    dc = np.clip(kc[:, None] - c[None, :] + 15, 0, 30)
    tab = np.empty((128, 2, 14, 64), np.float32)
    for hi, h in enumerate(heads):
        for s_ in range(14):
            dr = s_ + half
            g = rpb_l[h][dr[:, None], dc]
            tab[:, hi, s_, :] = np.where(valid, g, np.float32(NEG))
    return tab.reshape(128, -1)


def p2_inputs(inp, l, r1):
    maps = []
    lamv = np.stack([np.asarray(inp[k][l]) for k in ("lam_q1", "lam_k1", "lam_q2", "lam_k2")], 0).reshape(1, -1)
    lamv = np.ascontiguousarray(np.broadcast_to(lamv, (128, 256))).astype(np.float32)
    gs = np.ascontiguousarray(np.asarray(inp["subln_g"][l]).reshape(128, 1)).astype(np.float32)
    rpb_l = np.asarray(inp["rpb"][l])
    for c in range(8):
        b, j = c // 4, c % 4
        rows = slice(j * 128, (j + 1) * 128)
        cat = lambda key: np.ascontiguousarray(np.concatenate([r1[b * 4 + t][key][rows] for t in range(4)], axis=1))
        catv = lambda key: np.concatenate([r1[b * 4 + t][key][:, rows] for t in range(4)], axis=0)
        v = catv("va")
        nv = catv("vb")
        tok = lambda a: np.ascontiguousarray(a.reshape(-1, 128, 128).transpose(1, 0, 2).reshape(128, -1))
        maps.append({
            "q": cat("qaT"), "k": cat("kaT"), "v": tok(v),
            "nq": cat("qbT"), "nk": cat("kbT"), "nve": tok(nv), "nvo": tok(nv[64:64 + 63 * 128]),
            "tab": na_table(rpb_l, (2 * j, 2 * j + 1)), "lamv": lamv, "gs": gs,
        })
    return maps


def residual_ln_out(cx, mm_pairs_fn, reads, x_d, g, gb_, b, bb_, tmp, rings, out_d, xT, xTb, tt, tt_out=None):
    P = cx.P
    xt, xtb = rings["x"].next()
    cx.load(xt[:], xtb, x_d[tt * 128:(tt + 1) * 128, :])
    r, rb = rings["r"].next()
    for half in range(2):
        ps, psb = cx.ps.next()
        cx.mm(ps[:], psb, mm_pairs_fn(half), reads)
        hs = slice(half * 512, (half + 1) * 512)
        P.add("dve", lambda e, r=r, xt=xt, ps=ps, hs=hs: e.scalar_tensor_tensor(
            out=r[:, hs], in0=xt[:, hs], scalar=float(ALPHA), in1=ps[:], op0=ALU.mult, op1=ALU.add),
            reads=[xtb, psb], writes=[rb])
    o, ob = rings["o"].next()
    layer_norm_tile(cx, r, rb, g, gb_, b, bb_, o, ob, tmp)
    cx.store(out_d[tt * 128:(tt + 1) * 128, :], o[:], ob)
    if xT is not None:
        xb16, xb16b = rings["xbf"].next()
        P.add("dve", lambda e, o=o, xb16=xb16: e.tensor_copy(out=xb16[:], in_=o[:]), reads=[ob], writes=[xb16b])
        transpose_tile(cx, xb16, xb16b, xT, xTb, tt if tt_out is None else tt_out)


def std_rings(P):
    return {"x": P.ring("rx", 2, [128, D], F32), "r": P.ring("rr", 1, [128, D], F32),
            "o": P.ring("ro", 2, [128, D], F32), "xbf": P.ring("rxbf", 2, [128, D], BF16)}


def build_p3a(cx):
    P = cx.P
    x_d = cx.dram_in("x", [T, D], F32)
    xT_d = cx.dram_in("xT", [D, T], BF16)
    p_d = cx.dram_in("pTh", [512, T + 2], F32)
    gbT_d = cx.dram_in("gbT", [512, T], F32)
    cw_d = cx.dram_in("cw", [128, 12], F32)
    wg_d = cx.dram_in("wg", [D, 3072], F32)
    wb_d = cx.dram_in("wb", [3, 512, D], F32)
    wm_d = cx.dram_in("wm", [D, D], F32)
    g_d = cx.dram_in("g", [128, D], F32)
    b_d = cx.dram_in("b", [128, D], F32)
    x1_d = cx.dram_out("x1", [T, D], F32)
    x1T_d = cx.dram_out("x1T", [D, T], BF16)
    xT, xTb = P.tile("xT", [128, 8, T], BF16)
    cx.load(xT[:], xTb, xT_d.rearrange("(dc p) t -> p dc t", p=128))
    yT, yTb = P.tile("yT", [128, 12, T], BF16)
    yab, ybb, ycb = Buf(), Buf(), Buf()
    m2 = cx.bind["M2"].rearrange("(w j p) c -> j w p c", j=4, w=2)
    for w_, bb2 in ((0, yab), (1, ybb)):
        for j_ in range(4):
            cx.load(yT[:, w_ * 4 + j_, :], bb2, m2[j_, w_])
    cw, cwb = P.tile("cw", [128, 4, 3], F32)
    cx.load(cw[:], cwb, cw_d.rearrange("p (a b) -> p a b", b=3))
    g, gb_ = P.tile("g_s", [128, D], F32)
    b, bb_ = P.tile("b_s", [128, D], F32)
    cx.load(g[:], gb_, g_d)
    cx.load(b[:], bb_, b_d)
    pr = P.ring("pp", 2, [128, 514], F32)
    gr = P.ring("gq", 2, [128, 512], F32)
    ar = P.ring("acc", 2, [128, 512], F32)
    for c4 in range(4):
        rows = slice(c4 * 128, (c4 + 1) * 128)
        for qt in range(4):
            pt, ptb = pr.next()
            gt, gtb = gr.next()
            ac, acb = ar.next()
            cx.load(pt[:], ptb, p_d[rows, qt * 512:qt * 512 + 514])
            cx.load(gt[:], gtb, gbT_d[rows, qt * 512:(qt + 1) * 512])
            P.add("dve", lambda e, ac=ac, pt=pt, c4=c4: e.tensor_scalar(out=ac[:], in0=pt[:, 0:512], scalar1=cw[:, c4, 0:1], scalar2=None,
                                                                        op0=ALU.mult), reads=[ptb, cwb], writes=[acb])
            for k in (1, 2):
                P.add("dve", lambda e, ac=ac, pt=pt, c4=c4, k=k: e.scalar_tensor_tensor(
                    out=ac[:], in0=pt[:, k:k + 512], scalar=cw[:, c4, k:k + 1], in1=ac[:], op0=ALU.mult, op1=ALU.add),
                    reads=[ptb, cwb, acb], writes=[acb])
            P.add("pool", lambda e, ac=ac, gt=gt, c4=c4, qt=qt: e.tensor_tensor(
                out=yT[:, 8 + c4, qt * 512:(qt + 1) * 512], in0=ac[:], in1=gt[:], op=ALU.mult), reads=[acb, gtb], writes=[ycb])
    ybufs = [yab, ybb, ycb]
    mT, mTb = P.tile("mT", [128, 8, T], BF16)
    wgr = P.ring("wgc", 2, [128, 8, 3, 128], BF16)
    wbr = P.ring("wbc", 2, [128, 4, 3, 128], BF16)
    sgr = P.ring("sig", 3, [128, 512], F32)
    mar = P.ring("macc", 2, [128, 512], F32)
    tmr = P.ring("mtmp", 2, [128, 512], F32)
    wgv = wg_d.rearrange("(dc p) (br n) -> p dc br n", p=128, br=3)
    wbv = wb_d.rearrange("br (c p) n -> p c br n", p=128)
    for cc in range(8):
        cs = slice(cc * 128, (cc + 1) * 128)
        wgt, wgtb = wgr.next()
        wbt, wbtb = wbr.next()
        for br in range(3):
            cx.load(wgt[:, :, br, :], wgtb, wgv[:, :, br, cs], q="pool")
            cx.load(wbt[:, :, br, :], wbtb, wbv[:, :, br, cs], q="pool")
        for tq in range(4):
            ts = slice(tq * 512, (tq + 1) * 512)
            ma, mab = mar.next()
            for br in range(3):
                G, Gb = cx.ps.next()
                cx.mm(G[:], Gb, [(wgt[:, d, br, :], xT[:, d, ts]) for d in range(8)], [wgtb, xTb])
                sg, sgb = sgr.next()
                P.add("act", lambda e, sg=sg, G=G: e.activation(out=sg[:], in_=G[:], func=AF.Sigmoid), reads=[Gb], writes=[sgb])
                B, Bb = cx.ps.next()
                cx.mm(B[:], Bb, [(wbt[:, c4, br, :], yT[:, br * 4 + c4, ts]) for c4 in range(4)], [wbtb, ybufs[br]])
                if br == 0:
                    P.add("dve", lambda e, ma=ma, B=B, sg=sg: e.tensor_tensor(out=ma[:], in0=B[:], in1=sg[:], op=ALU.mult),
                          reads=[Bb, sgb], writes=[mab])
                else:
                    tm, tmb = tmr.next()
                    P.add("dve", lambda e, tm=tm, B=B, sg=sg: e.tensor_tensor(out=tm[:], in0=B[:], in1=sg[:], op=ALU.mult),
                          reads=[Bb, sgb], writes=[tmb])
                    if br == 1:
                        P.add("pool", lambda e, ma=ma, tm=tm: e.tensor_tensor(out=ma[:], in0=ma[:], in1=tm[:], op=ALU.add),
                              reads=[mab, tmb], writes=[mab])
                    else:
                        P.add("pool", lambda e, ma=ma, tm=tm, cc=cc, ts=ts: e.tensor_tensor(out=mT[:, cc, ts], in0=ma[:], in1=tm[:], op=ALU.add),
                              reads=[mab, tmb], writes=[mTb])
    wr = P.ring("wblk", 2, [128, 8, 512], BF16)
    wmv = wview(wm_d)
    wh = []
    for half in range(2):
        wt, wtb = wr.next()
        cx.load(wt[:], wtb, wmv[:, :, half * 512:(half + 1) * 512], q="pool")
        wh.append((wt, wtb))
    tmp = ln_tmp(P)
    rings = std_rings(P)
    for tt in range(NT):
        tsl = slice(tt * 128, (tt + 1) * 128)
        residual_ln_out(cx, lambda half, tsl=tsl: [(mT[:, c8, tsl], wh[half][0][:, c8, :]) for c8 in range(8)],
                        [mTb, wh[0][1], wh[1][1]], x_d, g, gb_, b, bb_, tmp, rings, x1_d, xT, xTb, tt)
    cx.store(x1T_d.rearrange("(dc p) t -> p dc t", p=128), xT[:], xTb)


def build_p3b(cx):
    P = cx.P
    x_d = cx.dram_in("x1", [T, D], F32)
    xT_d = cx.dram_in("x1T", [D, T], BF16)
    mem_d = cx.dram_in("mem", [256, D], F32)
    wq_d = cx.dram_in("wq", [D, D], F32)
    wkv_d = cx.dram_in("wkv", [D, 2 * D], F32)
    wo_d = cx.dram_in("wo", [D, D], F32)
    g_d = cx.dram_in("g", [128, D], F32)
    b_d = cx.dram_in("b", [128, D], F32)
    x2_d = cx.dram_out("x2", [T, D], F32)
    x2T_d = cx.dram_out("x2T", [D, T], BF16)
    xT, xTb = P.tile("xT", [128, 8, T], BF16)
    cx.load(xT[:], xTb, xT_d.rearrange("(dc p) t -> p dc t", p=128))
    g, gb_ = P.tile("g_s", [128, D], F32)
    b, bb_ = P.tile("b_s", [128, D], F32)
    cx.load(g[:], gb_, g_d)
    cx.load(b[:], bb_, b_d)
    mem32, mem32b = P.tile("mem32", [128, 2, D], F32)
    cx.load(mem32[:], mem32b, mem_d.rearrange("(mt p) d -> p mt d", p=128))
    mem16, mem16b = P.tile("mem16", [128, 2, D], BF16)
    P.add("dve", lambda e: e.tensor_copy(out=mem16[:], in_=mem32[:]), reads=[mem32b], writes=[mem16b])
    memT, memTb = P.tile("memT", [128, 8, 256], BF16)
    for mt in range(2):
        for half in range(2):
            ps, psb = cx.ps.next()

            def fn(e, ps=ps, half=half, mt=mt):
                ins = None
                for j in range(4):
                    d = half * 4 + j
                    ins = e.matmul(ps[:, j * 128:(j + 1) * 128], lhsT=mem16[:, mt, d * 128:(d + 1) * 128], rhs=cx.ident[:],
                                   start=True, stop=True)
                return ins

            P.add("pe", fn, reads=[mem16b, cx.identb], writes=[psb])
            P.add("act", lambda e, ps=ps, half=half, mt=mt: e.copy(
                out=memT[:, half * 4:half * 4 + 4, mt * 128:(mt + 1) * 128],
                in_=ps[:, 0:512].rearrange("p (j t) -> p j t", j=4)), reads=[psb], writes=[memTb])
    wr = P.ring("wblk", 3, [128, 8, 512], BF16)

    def load_w(w_d, i):
        wt, wtb = wr.next()
        cx.load(wt[:], wtb, wview(w_d)[:, :, i * 512:(i + 1) * 512], q="pool")
        return wt, wtb

    KxT, KxTb = P.tile("KxT", [128, 8, 256], BF16)
    Vx, Vxb = P.tile("Vx", [128, 2, D], BF16)
    for blk in range(2):
        wt, wtb = load_w(wkv_d, blk)
        for c4 in range(4):
            ps, psb = cx.ps.next()
            cx.mm(ps[:, 0:256], psb, [(wt[:, d, c4 * 128:(c4 + 1) * 128], memT[:, d, :]) for d in range(8)], [wtb, memTb])
            P.add("act", lambda e, ps=ps, blk=blk, c4=c4: e.copy(out=KxT[:, blk * 4 + c4, :], in_=ps[:, 0:256]), reads=[psb], writes=[KxTb])
    for blk in range(2):
        wt, wtb = load_w(wkv_d, 2 + blk)
        for mt in range(2):
            ps, psb = cx.ps.next()
            cx.mm(ps[:], psb, [(memT[:, d, mt * 128:(mt + 1) * 128], wt[:, d, :]) for d in range(8)], [wtb, memTb])
            P.add("act", lambda e, ps=ps, blk=blk, mt=mt: e.copy(out=Vx[:, mt, blk * 512:(blk + 1) * 512], in_=ps[:]), reads=[psb], writes=[Vxb])
    qxT, qxTb = P.tile("qxT", [128, 8, T], BF16)
    for blk in range(2):
        wt, wtb = load_w(wq_d, blk)
        for c4 in range(4):
            for tq in range(4):
                ts = slice(tq * 512, (tq + 1) * 512)
                ps, psb = cx.ps.next()
                cx.mm(ps[:], psb, [(wt[:, d, c4 * 128:(c4 + 1) * 128], xT[:, d, ts]) for d in range(8)], [wtb, xTb])
                P.add("act", lambda e, ps=ps, blk=blk, c4=c4, ts=ts: e.copy(out=qxT[:, blk * 4 + c4, ts], in_=ps[:]), reads=[psb], writes=[qxTb])
    oxT, oxTb = P.tile("oxT", [128, 8, T], BF16)
    ptr = P.ring("xpT", 4, [128, 512], BF16)
    rzr = P.ring("xrz", 2, [128, 512], F32)
    for hx in range(4):
        for tq in range(4):
            ts = slice(tq * 512, (tq + 1) * 512)
            pts = []
            for mt in range(2):
                s, sb = cx.ps.next()
                cx.mm(s[:], sb, [(KxT[:, 2 * hx + dc, mt * 128:(mt + 1) * 128], qxT[:, 2 * hx + dc, ts]) for dc in range(2)], [KxTb, qxTb])
                pt, ptb = ptr.next()
                P.add("act", lambda e, s=s, pt=pt: e.activation(out=pt[:], in_=s[:], func=AF.Exp, scale=1.0 / 16.0), reads=[sb], writes=[ptb])
                pts.append((pt, ptb))
            Z, Zb = cx.ps.next()
            cx.mm(Z[:], Zb, [(cx.ones[:], pts[mt][0][:]) for mt in range(2)], [cx.onesb, pts[0][1], pts[1][1]])
            rz, rzb = rzr.next()
            P.add("dve", lambda e, rz=rz, Z=Z: e.reciprocal(out=rz[:], in_=Z[:]), reads=[Zb], writes=[rzb])
            for dc in range(2):
                O, Ob = cx.ps.next()
                cx.mm(O[:], Ob, [(Vx[:, mt, (2 * hx + dc) * 128:(2 * hx + dc + 1) * 128], pts[mt][0][:]) for mt in range(2)],
                      [Vxb, pts[0][1], pts[1][1]])
                P.add("dve", lambda e, O=O, rz=rz, hx=hx, dc=dc, ts=ts: e.tensor_tensor(out=oxT[:, 2 * hx + dc, ts], in0=O[:], in1=rz[:], op=ALU.mult),
                      reads=[Ob, rzb], writes=[oxTb])
    wh = [load_w(wo_d, half) for half in range(2)]
    tmp = ln_tmp(P)
    rings = std_rings(P)
    for tt in range(NT):
        tsl = slice(tt * 128, (tt + 1) * 128)
        residual_ln_out(cx, lambda half, tsl=tsl: [(oxT[:, c8, tsl], wh[half][0][:, c8, :]) for c8 in range(8)],
                        [oxTb, wh[0][1], wh[1][1]], x_d, g, gb_, b, bb_, tmp, rings, x2_d, xT, xTb, tt)
    cx.store(x2T_d.rearrange("(dc p) t -> p dc t", p=128), xT[:], xTb)
    e3s, e3sb = P.tile("e3s", [128, 8, 2], F32)
    P.add("act", lambda e: e.copy(out=e3s[:, :, 0:1], in_=xT[:, :, 0:1]), reads=[xTb], writes=[e3sb])
    P.add("act", lambda e: e.copy(out=e3s[:, :, 1:2], in_=xT[:, :, T - 1:T]), reads=[xTb, e3sb], writes=[e3sb])
    cx.store(cx.bind["E3src"], e3s[:].rearrange("p a b -> p (a b)"), e3sb)


def build_p4(cx):
    P = cx.P
    TH = T // 2
    x_d = cx.dram_in("x2", [T, D], F32)
    xT_d = cx.dram_in("x2T", [D, T], BF16)
    wi_d = cx.dram_in("wi", [D, 2 * DFF], F32)
    cw_d = cx.dram_in("cwf", [128, NF * 4], F32)
    wo_d = cx.dram_in("wo", [DFF, D], F32)
    g_d = cx.dram_in("g", [128, D], F32)
    b_d = cx.dram_in("b", [128, D], F32)
    x3_d = cx.dram_out("x3", [T, D], F32)
    xT, xTb = P.tile("xT", [128, 8, T + 2], BF16)
    cx.load(xT[:, :, 1:T + 1], xTb, xT_d.rearrange("(dc p) t -> p dc t", p=128))
    halo_fix(cx, cx.bind["E3"], 8, F32, None, *cx.bind["masks"], sb_target=(xT, xTb))
    g, gb_ = P.tile("g_s", [128, D], F32)
    b, bb_ = P.tile("b_s", [128, D], F32)
    cx.load(g[:], gb_, g_d)
    cx.load(b[:], bb_, b_d)
    cw, cwb = P.tile("cwf", [128, NF, 4], F32)
    cx.load(cw[:], cwb, cw_d.rearrange("p (a b) -> p a b", b=4))
    Wo, Wob = P.tile("Wo", [128, NF, D], BF16)
    wov = wo_d.rearrange("(f p) n -> p f n", p=128)
    for i in range(2):
        cx.load(Wo[:, i * 11:(i + 1) * 11, :], Wob, wov[:, i * 11:(i + 1) * 11, :], q="pool")
    hT, hTb = P.tile("hT", [128, NF, TH], BF16)
    wur = P.ring("wu", 2, [128, 8, 128], BF16)
    wgr = P.ring("wgt", 2, [128, 8, 128], BF16)
    gtr = P.ring("gt", 2, [128, TH + 2], F32)
    acr = P.ring("facc", 2, [128, TH], F32)
    slr = P.ring("fsil", 2, [128, TH], F32)
    wiv = wview(wi_d)
    tmp = ln_tmp(P)
    rings = std_rings(P)
    for th in range(2):
        c0 = th * TH
        for f in range(NF):
            wu, wub = wur.next()
            wg, wgb = wgr.next()
            cx.load(wu[:], wub, wiv[:, :, f * 128:(f + 1) * 128], q="pool")
            cx.load(wg[:], wgb, wiv[:, :, DFF + f * 128:DFF + (f + 1) * 128], q="pool")
            U = []
            for tq in range(2):
                ps, psb = cx.ps.next()
                cs = slice(c0 + 1 + tq * 512, c0 + 1 + (tq + 1) * 512)
                cx.mm(ps[:], psb, [(wu[:, d, :], xT[:, d, cs]) for d in range(8)], [wub, xTb])
                U.append((ps, psb))
            gt, gtb = gtr.next()
            for (a0, n) in ((0, 512), (512, 512), (1024, 2)):
                ps, psb = cx.ps.next()
                cx.mm(ps[:, 0:n], psb, [(wg[:, d, :], xT[:, d, c0 + a0:c0 + a0 + n]) for d in range(8)], [wgb, xTb])
                P.add("act", lambda e, ps=ps, gt=gt, a0=a0, n=n: e.copy(out=gt[:, a0:a0 + n], in_=ps[:, 0:n]), reads=[psb], writes=[gtb])
            ac, acb = acr.next()
            P.add("dve", lambda e, ac=ac, gt=gt, f=f: e.tensor_scalar(out=ac[:], in0=gt[:, 0:TH], scalar1=cw[:, f, 0:1], scalar2=cw[:, f, 3:4],
                                                                      op0=ALU.mult, op1=ALU.add), reads=[gtb, cwb], writes=[acb])
            for k in (1, 2):
                P.add("dve", lambda e, ac=ac, gt=gt, f=f, k=k: e.scalar_tensor_tensor(
                    out=ac[:], in0=gt[:, k:k + TH], scalar=cw[:, f, k:k + 1], in1=ac[:], op0=ALU.mult, op1=ALU.add),
                    reads=[gtb, cwb, acb], writes=[acb])
            sl, slb = slr.next()
            P.add("act", lambda e, sl=sl, ac=ac: e.activation(out=sl[:], in_=ac[:], func=AF.Silu), reads=[acb], writes=[slb])
            for tq in range(2):
                ps, psb = U[tq]
                P.add("dve", lambda e, ps=ps, sl=sl, f=f, tq=tq: e.tensor_tensor(
                    out=hT[:, f, tq * 512:(tq + 1) * 512], in0=ps[:], in1=sl[:, tq * 512:(tq + 1) * 512], op=ALU.mult),
                    reads=[psb, slb], writes=[hTb])
        for t8 in range(NT // 2):
            tt = th * (NT // 2) + t8
            tsl = slice(t8 * 128, (t8 + 1) * 128)
            residual_ln_out(cx, lambda half, tsl=tsl: [(hT[:, f, tsl], Wo[:, f, half * 512:(half + 1) * 512]) for f in range(NF)],
                            [hTb, Wob], x_d, g, gb_, b, bb_, tmp, rings, x3_d, None, None, tt)


class FMView:
    def __init__(self, g1s, which):
        self.g1s = g1s
        self.which = which

    def __getitem__(self, idx):
        rows, cols = idx
        j = rows.start // 128
        return self.g1s[j, self.which, :, cols]


def halo_fix(cx, E_ap, n, dtype, target, mL, mLb, mR, mRb, sb_target=None, e_reads=()):
    P = cx.P
    e, eb = P.tile("he", [128, 4, n, 2], dtype)
    cx.load(e[:].rearrange("p t a b -> p t (a b)"), eb, E_ap.rearrange("(t p) x -> p t x", p=128), reads=list(e_reads))
    tv = target.rearrange("(c p) t -> p c t", p=128) if target is not None else None
    for (m, mb, src_col, dst_col, nm) in ((mL, mLb, 1, 0, "hl"), (mR, mRb, 0, T + 1, "hr")):
        acc, accb = P.tile(nm + "a", [128, n, 1], F32)
        out, outb = P.tile(nm + "o", [128, n, 1], dtype)
        P.add("dve", lambda e_, acc=acc, m=m, sc=src_col: e_.tensor_scalar(out=acc[:], in0=e[:, 0, :, sc:sc + 1], scalar1=m[:, 0:1], scalar2=None,
                                                                       op0=ALU.mult), reads=[eb, mb], writes=[accb])
        for s_ in range(1, 4):
            P.add("dve", lambda e_, acc=acc, m=m, sc=src_col, s_=s_: e_.scalar_tensor_tensor(
                out=acc[:], in0=e[:, s_, :, sc:sc + 1], scalar=m[:, s_:s_ + 1], in1=acc[:], op0=ALU.mult, op1=ALU.add),
                reads=[eb, mb, accb], writes=[accb])
        if sb_target is not None:
            tt_, ttb_ = sb_target
            P.add("dve", lambda e_, acc=acc, tt_=tt_, dc=dst_col: e_.tensor_copy(out=tt_[:, :, dc:dc + 1], in_=acc[:]), reads=[accb], writes=[ttb_])
        else:
            P.add("dve", lambda e_, acc=acc, out=out: e_.tensor_copy(out=out[:], in_=acc[:]), reads=[accb], writes=[outb])
            cx.store(tv[:, :, dst_col:dst_col + 1], out[:], outb, slow=True)


def build_fused(stop=99):
    nc = bass.Bass("TRN2", target_bir_lowering=False)
    cx = Ctx(nc)
    P = cx.P
    cx.consts()
    mL_d = cx.dram_in("mL", [128, 4], F32)
    mR_d = cx.dram_in("mR", [128, 4], F32)
    mL, mLb = P.tile("mL", [128, 4], F32)
    mR, mRb = P.tile("mR", [128, 4], F32)
    cx.load(mL[:], mLb, mL_d)
    cx.load(mR[:], mRb, mR_d)
    sc = cx.scratch
    G1src = sc("G1src", [4 * 6 * 128, T], BF16)
    G1 = sc("G1", [4 * 4 * 6 * 128, T], BF16)
    M1 = sc("M1", [4 * 6 * 128, T], BF16)
    M2 = sc("M2", [4 * 2 * 128, T], BF16)
    G2src = sc("G2src", [4 * 2 * 128, T], BF16)
    G2 = sc("G2", [4 * 4 * 2 * 128, T], BF16)
    E1src = sc("E1src", [128, 8], F32)
    E1 = sc("E1", [512, 8], F32)
    E3src = sc("E3src", [128, 16], F32)
    E3 = sc("E3", [512, 16], F32)
    xres = sc("xres", [T, D], F32)
    x3s = sc("x3s", [T, D], F32)
    xTs = sc("xTs", [D, T], BF16)
    pTh = sc("pThs", [512, T + 2], F32)
    gbTs = sc("gbTs", [512, T], F32)
    x1s = sc("x1s", [T, D], F32)
    x1Ts = sc("x1Ts", [D, T], BF16)
    x2s = sc("x2s", [T, D], F32)
    x2Ts = sc("x2Ts", [D, T], BF16)
    out_d = nc.dram_tensor("out", [T, D], F32, kind="ExternalOutput").ap()
    g1s = G1src.rearrange("(j w p) c -> j w p c", j=4, w=6)

    def vview(which):
        return lambda tt: g1s[:, which, :, tt * 128:(tt + 1) * 128].rearrange("j p c -> p j c")

    for l in range(DEPTH):
        lam_init = 0.8 - 0.6 * math.exp(-0.3 * l)
        first = (l == 0)
        xin = xres if first else x3s
        cx.suffix = f"_p1_{l}"
        cx.bind = {"xln": xres, "xT": xTs, "qaT": FMView(g1s, 0), "kaT": FMView(g1s, 1), "qbT": FMView(g1s, 3), "kbT": FMView(g1s, 4),
                   "va": vview(2), "vb": vview(5), "gbT": gbTs, "pT": pTh[:, 1:T + 1], "E1src": E1src}
        if not first:
            cx.bind["x"] = x3s
        gA, gB, e1g, m1A, m1B = Buf(), Buf(), Buf(), Buf(), Buf()

        def setA_hook(stores, gA=gA):
            for w_ in range(3):
                for j_ in range(4):
                    blk = j_ * 6 + w_
                    cx.allgather(G1src[blk * 128:(blk + 1) * 128, :], G1[blk * 512:(blk + 1) * 512, :], writes=[gA], after=stores)

        cx.bind["setA_hook"] = setA_hook
        with cx.phase(f"L{l}p1_"):
            build_p1(cx, first)
        if stop == 1:
            break
        P.barrier()
        cx.allgather(E1src, E1, writes=[e1g])
        for w_ in range(3, 6):
            for j_ in range(4):
                blk = j_ * 6 + w_
                cx.allgather(G1src[blk * 128:(blk + 1) * 128, :], G1[blk * 512:(blk + 1) * 512, :], writes=[gB])
        g1v = G1.rearrange("(j r) c -> j r c", j=4)
        cx.load_dyn(M1[0:1536, :].rearrange("(a r) c -> a r c", a=1), m1A,
                    lambda e: g1v[bass.ds(cx.rank(e), 1), 0:1536, :], reads=[gA])

        def m1b_hook():
            cx.load_dyn(M1[1536:3072, :].rearrange("(a r) c -> a r c", a=1), m1B,
                        lambda e: g1v[bass.ds(cx.rank(e), 1), 1536:3072, :], reads=[gB])

        def ya_hook(stores):
            for blk in range(0, 8, 2):
                cx.allgather(G2src[blk * 128:(blk + 1) * 128, :], G2[blk * 512:(blk + 1) * 512, :], after=stores)

        cx.bind = {"M1": M1, "G2src": G2src, "m1A": m1A, "m1B": m1B, "m1b_hook": m1b_hook, "e1g": e1g, "ya_hook": ya_hook}
        cx.suffix = f"_p2_{l}"
        with cx.phase(f"L{l}p2_"):
            halo_fix(cx, E1, 4, F32, pTh, mL, mLb, mR, mRb, e_reads=[e1g])
            build_p2(cx, lam_init)
        if stop == 5:
            break
        P.barrier()
        for blk in range(1, 8, 2):
            cx.allgather(G2src[blk * 128:(blk + 1) * 128, :], G2[blk * 512:(blk + 1) * 512, :])
        P.barrier()
        if stop == 6:
            break
        dumb = Buf()
        cx.load_dyn(M2.rearrange("(a r) c -> a r c", a=1), dumb,
                    lambda e: G2.rearrange("(t r) c -> t r c", t=4)[bass.ds(cx.rank(e), 1), :, :])
        P.barrier()
        cx.suffix = f"_p3a_{l}"
        cx.bind = {"x": xin, "xT": xTs, "M2": M2, "pTh": pTh, "gbT": gbTs, "x1": x1s, "x1T": x1Ts}
        with cx.phase(f"L{l}p3a_"):
            build_p3a(cx)
        if stop == 7:
            break
        cx.suffix = f"_p3b_{l}"
        cx.bind = {"x1": x1s, "x1T": x1Ts, "x2": x2s, "x2T": x2Ts, "E3src": E3src}
        with cx.phase(f"L{l}p3b_"):
            build_p3b(cx)
        if stop == 8:
            break
        P.barrier()
        cx.allgather(E3src, E3)
        P.barrier()
        if stop == 9:
            break
        cx.suffix = f"_p4_{l}"
        cx.bind = {"x2": x2s, "x2T": x2Ts, "E3": E3, "masks": (mL, mLb, mR, mRb), "x3": x3s if l < DEPTH - 1 else out_d}
        with cx.phase(f"L{l}p4_"):
            build_p4(cx)
        if stop == 10:
            break
    if stop < 99:
        P.barrier()
        for nm, ap in (("xres", xres), ("M1", M1), ("pThs", pTh), ("G2src", G2src), ("M2", M2), ("x1s", x1s), ("x2s", x2s), ("x3s", x3s), ("E1", E1), ("E3", E3), ("E3src", E3src)):
            shp = list(ap.shape)
            dd = nc.dram_tensor("dbg_" + nm, shp, ap.tensor.dtype, kind="ExternalOutput").ap()
            bb = Buf()
            P.add("sp", lambda e, dd=dd, ap=ap: e.dma_start(out=dd, in_=ap), writes=[bb], dma=True)
            cx.outs.append(bb)
    cx.done()
    return nc


def _bc(v):
    return np.ascontiguousarray(np.broadcast_to(np.asarray(v, np.float32)[None, :], (128, D)))


def host_inputs(inp):
    common = {}
    per_core = [dict() for _ in range(8)]
    for l in range(DEPTH):
        w_in = np.asarray(inp["w_in"][l])
        common[f"w_p1_{l}"] = p1_weight(w_in)
        if l == 0:
            common["g_p1_0"] = _bc(inp["emb_ln_g"])
            common["b_p1_0"] = _bc(inp["emb_ln_b"])
        lamv = np.stack([np.asarray(inp[k][l]) for k in ("lam_q1", "lam_k1", "lam_q2", "lam_k2")], 0).reshape(1, -1)
        common[f"lamv_p2_{l}"] = np.ascontiguousarray(np.broadcast_to(lamv, (128, 256))).astype(np.float32)
        common[f"gs_p2_{l}"] = np.ascontiguousarray(np.asarray(inp["subln_g"][l]).reshape(128, 1)).astype(np.float32)
        common[f"wg_p3a_{l}"] = np.ascontiguousarray(w_in[:, 4608:7680])
        common[f"wb_p3a_{l}"] = np.ascontiguousarray(np.asarray(inp["w_branch"][l]))
        common[f"wm_p3a_{l}"] = np.ascontiguousarray(np.asarray(inp["w_mix_out"][l]))
        cwl = np.asarray(inp["sc_conv_w"][l])
        common[f"cw_p3a_{l}"] = np.ascontiguousarray(cwl.reshape(3, 4, 128).transpose(2, 1, 0).reshape(128, 12))
        common[f"g_p3a_{l}"] = _bc(inp["ln_g"][l, 0])
        common[f"b_p3a_{l}"] = _bc(inp["ln_b"][l, 0])
        common[f"wq_p3b_{l}"] = np.ascontiguousarray(np.asarray(inp["xa_q"][l]))
        common[f"wkv_p3b_{l}"] = np.ascontiguousarray(np.asarray(inp["xa_kv"][l]))
        common[f"wo_p3b_{l}"] = np.ascontiguousarray(np.asarray(inp["xa_o"][l]))
        common[f"g_p3b_{l}"] = _bc(inp["ln_g"][l, 1])
        common[f"b_p3b_{l}"] = _bc(inp["ln_b"][l, 1])
        common[f"wi_p4_{l}"] = np.ascontiguousarray(np.asarray(inp["ffn_w_in"][l]))
        common[f"wo_p4_{l}"] = np.ascontiguousarray(np.asarray(inp["ffn_w_out"][l]))
        cwf = np.concatenate([np.asarray(inp["ffn_conv_w"][l]), np.asarray(inp["ffn_conv_b"][l])[None, :]], axis=0)
        common[f"cwf_p4_{l}"] = np.ascontiguousarray(cwf.reshape(4, NF, 128).transpose(2, 1, 0).reshape(128, NF * 4))
        common[f"g_p4_{l}"] = _bc(inp["ln_g"][l, 2])
        common[f"b_p4_{l}"] = _bc(inp["ln_b"][l, 2])
        rpb_l = np.asarray(inp["rpb"][l])
        for c in range(8):
            j = c % 4
            per_core[c][f"tab_p2_{l}"] = na_table(rpb_l, (2 * j, 2 * j + 1))
            cosT, sinT = rope_tables(j)
            per_core[c][f"cosT_p1_{l}"] = cosT
            per_core[c][f"sinT_p1_{l}"] = sinT
    maps = []
    for c in range(8):
        b, tb = c // 4, c % 4
        m = dict(common)
        m.update(per_core[c])
        m["x_p1_0"] = np.ascontiguousarray(np.asarray(inp["x"])[b, tb * T:(tb + 1) * T])
        for l in range(DEPTH):
            m[f"mem_p3b_{l}"] = np.ascontiguousarray(np.asarray(inp["mem"])[b])
        mL = np.zeros((128, 4), np.float32)
        mR = np.zeros((128, 4), np.float32)
        if tb > 0:
            mL[:, tb - 1] = 1.0
        if tb < 3:
            mR[:, tb + 1] = 1.0
        m["mL"] = mL
        m["mR"] = mR
        maps.append(m)
    return maps


_NC = []


def kernel(**inp):
    inp = {k: np.asarray(v) for k, v in inp.items()}
    if not _NC:
        _NC.append(build_fused())
    res = run_bass_kernel_spmd(_NC[0], host_inputs(inp), core_ids=list(range(8))).results
    out = np.stack([np.concatenate([res[b * 4 + t]["out"] for t in range(4)], axis=0) for b in range(2)], 0)
    return np.ascontiguousarray(out.astype(np.float32))
```

```python
import contextlib
import math
import numpy as np
import ml_dtypes
import concourse.bass as bass
import concourse.mybir as mybir
from concourse.bass_utils import run_bass_kernel_spmd

F32 = mybir.dt.float32
BF16 = mybir.dt.bfloat16
ALU = mybir.AluOpType
AF = mybir.ActivationFunctionType
NPBF = ml_dtypes.bfloat16

D = 1024
S = 8192
T = 2048
NT = 16
DEPTH = 2
DFF = 2816
NF = 22
EPS = 1e-5
ALPHA = (2.0 * DEPTH) ** 0.25
NEG = -30000.0

SEM_CHUNK = 1000
N_DMA_SEMS = 8
N_CC_SEMS = 16


class Buf:
    __slots__ = ("w", "rs", "rd")

    def __init__(self):
        self.w = None
        self.rs = {}
        self.rd = []


class Op:
    __slots__ = ("eng", "fn", "deps", "idx", "is_dma", "sig", "sem", "val", "prev", "cc")

    def __init__(self, eng, fn, is_dma):
        self.eng = eng
        self.fn = fn
        self.is_dma = is_dma
        self.deps = []
        self.sig = is_dma
        self.sem = None
        self.val = 0
        self.prev = None
        self.idx = 0
        self.cc = False


class _Dummy:
    def then_inc(self, *a, **k):
        return self


class Prog:
    ENGS = ("pe", "act", "dve", "pool", "sp")

    def __init__(self, nc):
        self.nc = nc
        self.ops = {e: [] for e in self.ENGS}
        self.stack = contextlib.ExitStack()
        self.n = 0
        self.prefix = ""
        self.pstack = None
        self.epoch = 0
        self.bar_deps = []
        self.bars = []
        self.eng_epoch = {e: 0 for e in self.ENGS}
        self.since = []

    def sbuf(self, name, shape, dtype):
        st = self.pstack if self.pstack is not None else self.stack
        return st.enter_context(self.nc.sbuf_tensor("sb_" + self.prefix + name, list(shape), dtype))

    def barrier(self):
        deps = list(self.since)
        for e in self.ENGS:
            for op in reversed(self.ops[e]):
                if not op.is_dma:
                    deps.append(op)
                    break
        for d in deps:
            d.sig = True
        self.epoch += 1
        self.bars.append((self.epoch, deps))
        self.since = []

    def psum(self, name, shape, dtype=F32):
        return self.stack.enter_context(self.nc.psum_tensor("pp_" + name, list(shape), dtype))

    def tile(self, name, shape, dtype):
        return (self.sbuf(name, shape, dtype), Buf())

    def ring(self, name, n, shape, dtype):
        return Ring([self.tile(f"{name}{i}", shape, dtype) for i in range(n)])

    def add(self, eng, fn, reads=(), writes=(), dma=False):
        op = Op(eng, fn, dma)
        deps = {}

        def need(d):
            if d is None or d is op:
                return
            if (not d.is_dma) and d.eng == eng and eng == "pe":
                return
            deps[id(d)] = d

        for b in reads:
            need(b.w)
        for b in writes:
            need(b.w)
            for d in b.rs.values():
                need(d)
            for d in b.rd:
                need(d)
        best = {}
        out = []
        for d in deps.values():
            if d.is_dma:
                out.append(d)
            else:
                k = d.eng
                if k not in best or best[k].idx < d.idx:
                    best[k] = d
        out.extend(best.values())
        if self.eng_epoch[eng] < self.epoch:
            have = {id(d) for d in out}
            for (ep, bdeps) in self.bars:
                if ep > self.eng_epoch[eng]:
                    for d in bdeps:
                        if id(d) not in have and d is not op and not ((not d.is_dma) and d.eng == eng and eng == "pe"):
                            have.add(id(d))
                            out.append(d)
            self.eng_epoch[eng] = self.epoch
        if dma:
            self.since.append(op)
        op.deps = out
        for d in out:
            d.sig = True
        op.idx = len(self.ops[eng])
        self.ops[eng].append(op)
        for b in reads:
            if dma:
                b.rd.append(op)
            else:
                b.rs[eng] = op
        for b in writes:
            b.w = op
            b.rs = {}
            b.rd = []
        return op

    def finish(self, out_bufs):
        op = Op("sp", lambda eng: _Dummy(), False)
        op.deps = [b.w for b in out_bufs if b.w is not None]
        op.idx = len(self.ops["sp"])
        self.ops["sp"].append(op)

    def emit(self):
        nc = self.nc
        st = self.stack
        for e in self.ENGS:
            k = 0
            j = 0
            esems = []
            dsems = []
            lastd = {}
            csems = []
            lastc = {}
            jc = 0
            for op in self.ops[e]:
                if op.cc:
                    si = jc % N_CC_SEMS
                    if si >= len(csems):
                        csems.append(st.enter_context(nc.semaphore(f"c_{e}_{si}")))
                    op.sem = csems[si]
                    op.val = jc // N_CC_SEMS + 1
                    op.prev = lastc.get(si)
                    lastc[si] = op
                    jc += 1
                elif op.is_dma:
                    si = j % N_DMA_SEMS
                    if si >= len(dsems):
                        dsems.append(st.enter_context(nc.semaphore(f"d_{e}_{si}")))
                    op.sem = dsems[si]
                    op.val = 16 * (j // N_DMA_SEMS + 1)
                    op.prev = lastd.get(si)
                    lastd[si] = op
                    j += 1
                elif op.sig:
                    ci = k // SEM_CHUNK
                    if ci >= len(esems):
                        esems.append(st.enter_context(nc.semaphore(f"s_{e}_{ci}")))
                    op.sem = esems[ci]
                    op.val = k % SEM_CHUNK + 1
                    k += 1
        block = st.enter_context(nc.Block())
        prog = self

        def run(e, eng):
            waited = {}
            for op in prog.ops[e]:
                needs = {}
                dl = list(op.deps)
                if op.prev is not None:
                    dl.append(op.prev)
                for d in dl:
                    key = id(d.sem)
                    if key not in needs or needs[key][1] < d.val:
                        needs[key] = (d.sem, d.val)
                for key, (sem, val) in needs.items():
                    if waited.get(key, 0) < val:
                        eng.wait_ge(sem, val)
                        waited[key] = val
                ins = op.fn(eng)
                if op.sig:
                    ins.then_inc(op.sem, 16 if (op.is_dma and not op.cc) else 1)

        if self.ops["pe"]:
            @block.tensor
            def _(eng):
                run("pe", eng)
        if self.ops["act"]:
            @block.scalar
            def _(eng):
                run("act", eng)
        if self.ops["dve"]:
            @block.vector
            def _(eng):
                run("dve", eng)
        if self.ops["pool"]:
            @block.gpsimd
            def _(eng):
                run("pool", eng)
        if self.ops["sp"]:
            @block.sync
            def _(eng):
                run("sp", eng)

    def close(self):
        self.stack.close()


class Ring:
    def __init__(self, items):
        self.items = items
        self.i = 0

    def next(self):
        it = self.items[self.i % len(self.items)]
        self.i += 1
        return it


class Ctx:
    def __init__(self, nc):
        self.nc = nc
        self.P = Prog(nc)
        P = self.P
        self.ps = Ring([(P.psum(f"ps{i}", [128, 512]), Buf()) for i in range(8)])
        self.outs = []
        self.bind = {}
        self.suffix = ""
        self._rank = None
        self.last_stores = []

    def dram_in(self, name, shape, dtype):
        if name in self.bind:
            return self.bind[name]
        return self.nc.dram_tensor(name + self.suffix, list(shape), dtype, kind="ExternalInput").ap()

    def dram_out(self, name, shape, dtype):
        if name in self.bind:
            return self.bind[name]
        return self.nc.dram_tensor(name + self.suffix, list(shape), dtype, kind="ExternalOutput").ap()

    def scratch(self, name, shape, dtype):
        return self.nc.dram_tensor(name, list(shape), dtype, kind="Internal").ap()

    def rank(self, e):
        if self._rank is None:
            self._rank = e.partition_id() % 4
        return self._rank

    def load_dyn(self, dst_ap, dst_buf, src_fn, q="sp", reads=()):
        self.P.add(q, lambda e: e.dma_start(out=dst_ap, in_=src_fn(e)), reads=list(reads), writes=[dst_buf], dma=True)

    def allgather(self, src, dst, writes=(), after=()):
        groups = [[0, 1, 2, 3], [4, 5, 6, 7]]
        op = self.P.add("pool", lambda e: e.collective_compute("AllGather", ALU.bypass, replica_groups=groups, ins=[src], outs=[dst]),
                        writes=list(writes), dma=True)
        have = {id(d) for d in op.deps}
        for d in after:
            if id(d) not in have:
                op.deps.append(d)
                d.sig = True
        op.cc = True
        return op

    @contextlib.contextmanager
    def phase(self, prefix):
        self.P.prefix = prefix
        self.P.pstack = contextlib.ExitStack()
        try:
            yield
        finally:
            self.P.barrier()
            self.P.pstack.close()
            self.P.pstack = None

    def load(self, dst_ap, dst_buf, src_ap, q="sp", reads=()):
        self.P.add(q, lambda e: e.dma_start(out=dst_ap, in_=src_ap), reads=list(reads), writes=[dst_buf], dma=True)

    def store(self, dst_ap, src_ap, src_buf, q="sp", slow=False):
        b = Buf()
        if slow:
            op = self.P.add(q, lambda e: e.dma_start(out=dst_ap, in_=src_ap, allow_slow_non_contiguous=True), reads=[src_buf], writes=[b], dma=True)
        else:
            op = self.P.add(q, lambda e: e.dma_start(out=dst_ap, in_=src_ap), reads=[src_buf], writes=[b], dma=True)
        self.last_stores.append(op)
        self.outs.append(b)
        return op

    def consts(self):
        P = self.P
        self.ident, self.identb = P.tile("ident", [128, 128], BF16)
        ident = self.ident
        P.add("pool", lambda e: e.memset(ident[:], 1.0), writes=[self.identb])
        P.add("pool", lambda e: e.affine_select(out=ident[:], in_=ident[:], pattern=[[-1, 128]],
                                                compare_op=ALU.is_equal, fill=0.0, base=0, channel_multiplier=1),
              reads=[self.identb], writes=[self.identb])
        self.ones, self.onesb = P.tile("ones", [128, 128], BF16)
        ones = self.ones
        P.add("pool", lambda e: e.memset(ones[:], 1.0), writes=[self.onesb])

    def mm(self, ps_ap, psb, pairs, reads):
        pairs = list(pairs)

        def fn(e):
            n = len(pairs)
            ins = None
            for i, (l, r) in enumerate(pairs):
                ins = e.matmul(ps_ap, lhsT=l, rhs=r, start=(i == 0), stop=(i == n - 1))
            return ins

        self.P.add("pe", fn, reads=reads, writes=[psb])

    def done(self):
        self.P.finish(self.outs)
        self.P.emit()
        self.P.close()


def wview(w_ap):
    return w_ap.rearrange("(kc p) n -> p kc n", p=128)


def layer_norm_tile(cx, r, rb, g, gb_, b, bb_, out, outb, tmp):
    P = cx.P
    st, stb = tmp["st"].next()
    mv, mvb = tmp["mv"].next()
    P.add("dve", lambda e: e.bn_stats(out=st[:, 0:6], in_=r[:, 0:512]), reads=[rb], writes=[stb])
    P.add("dve", lambda e: e.bn_stats(out=st[:, 6:12], in_=r[:, 512:1024]), reads=[rb, stb], writes=[stb])
    P.add("dve", lambda e: e.bn_aggr(out=mv[:, 0:2], in_=st[:, 0:12]), reads=[stb], writes=[mvb])
    P.add("act", lambda e: e.activation(out=mv[:, 2:3], in_=mv[:, 1:2], func=AF.Sqrt, bias=EPS, scale=1.0),
          reads=[mvb], writes=[mvb])
    P.add("dve", lambda e: e.reciprocal(out=mv[:, 3:4], in_=mv[:, 2:3]), reads=[mvb], writes=[mvb])
    P.add("dve", lambda e: e.tensor_scalar(out=out[:], in0=r[:], scalar1=mv[:, 0:1], scalar2=mv[:, 3:4],
                                           op0=ALU.subtract, op1=ALU.mult), reads=[rb, mvb], writes=[outb])
    P.add("pool", lambda e: e.tensor_tensor(out=out[:], in0=out[:], in1=g[:], op=ALU.mult),
          reads=[outb, gb_], writes=[outb])
    P.add("pool", lambda e: e.tensor_tensor(out=out[:], in0=out[:], in1=b[:], op=ALU.add),
          reads=[outb, bb_], writes=[outb])


def ln_tmp(P, tag=""):
    return {"st": P.ring("lnst" + tag, 3, [128, 12], F32), "mv": P.ring("lnmv" + tag, 3, [128, 4], F32)}


def bcast_row(ap_1d, n):
    return ap_1d.rearrange("(o n) -> o n", o=1).broadcast(0, 128) if hasattr(ap_1d, "broadcast") else None


def transpose_tile(cx, xb16, xb16b, xT, xTb, tt):
    for half in range(2):
        ps, psb = cx.ps.next()
        pairs = []

        def fn(e, ps=ps, half=half):
            ins = None
            for j in range(4):
                d = half * 4 + j
                ins = e.matmul(ps[:, j * 128:(j + 1) * 128], lhsT=xb16[:, d * 128:(d + 1) * 128], rhs=cx.ident[:],
                               start=True, stop=True)
            return ins

        cx.P.add("pe", fn, reads=[xb16b, cx.identb], writes=[psb])
        cx.P.add("act", lambda e, ps=ps, half=half: e.copy(
            out=xT[:, half * 4:half * 4 + 4, tt * 128:(tt + 1) * 128],
            in_=ps[:, 0:512].rearrange("p (j t) -> p j t", j=4)), reads=[psb], writes=[xTb])


P1_BLOCKS = ["qa", "qap", "ka", "kap", "qb", "kb", "gb", "gc", "hc", "va", "vb"]


def build_p1(cx, first):
    P = cx.P
    x_d = cx.dram_in("x", [T, D], F32)
    w_d = cx.dram_in("w", [D, 11 * 512], F32)
    cos_d = cx.dram_in("cosT", [128, T], F32)
    sin_d = cx.dram_in("sinT", [128, T], F32)
    if first:
        g_d = cx.dram_in("g", [128, D], F32)
        b_d = cx.dram_in("b", [128, D], F32)
        xln_d = cx.dram_out("xln", [T, D], F32)
    xT_d = cx.dram_out("xT", [D, T], BF16)
    qaT_d = cx.dram_out("qaT", [512, T], BF16)
    kaT_d = cx.dram_out("kaT", [512, T], BF16)
    qbT_d = cx.dram_out("qbT", [512, T], BF16)
    kbT_d = cx.dram_out("kbT", [512, T], BF16)
    va_d = cx.dram_out("va", [T, 512], BF16)
    vb_d = cx.dram_out("vb", [T, 512], BF16)
    gbT_d = cx.dram_out("gbT", [512, T], F32)
    pT_d = cx.dram_out("pT", [512, T], F32)

    xT, xTb = P.tile("xT", [128, 8, T], BF16)
    cosT, cosb = P.tile("cosT_s", [128, T], F32)
    sinT, sinb = P.tile("sinT_s", [128, T], F32)
    cx.load(cosT[:], cosb, cos_d)
    cx.load(sinT[:], sinb, sin_d)
    if first:
        g, gb_ = P.tile("g_s", [128, D], F32)
        b, bb_ = P.tile("b_s", [128, D], F32)
        cx.load(g[:], gb_, g_d)
        cx.load(b[:], bb_, b_d)
        tmp = ln_tmp(P)
    xin = P.ring("xin", 2, [128, D], F32)
    xo = P.ring("xo", 2, [128, D], F32)
    xbf = P.ring("xbf", 2, [128, D], BF16)
    for tt in range(NT):
        xt, xtb = xin.next()
        cx.load(xt[:], xtb, x_d[tt * 128:(tt + 1) * 128, :])
        if first:
            o, ob = xo.next()
            layer_norm_tile(cx, xt, xtb, g, gb_, b, bb_, o, ob, tmp)
            cx.store(xln_d[tt * 128:(tt + 1) * 128, :], o[:], ob)
        else:
            o, ob = xt, xtb
        xb16, xb16b = xbf.next()
        P.add("dve", lambda e, o=o, xb16=xb16: e.tensor_copy(out=xb16[:], in_=o[:]), reads=[ob], writes=[xb16b])
        transpose_tile(cx, xb16, xb16b, xT, xTb, tt)
    cx.store(xT_d.rearrange("(dc p) t -> p dc t", p=128), xT[:], xTb)

    wr = P.ring("wblk", 11, [128, 8, 512], BF16)
    wv = wview(w_d)

    wpre = {}
    for i in (0, 1, 2, 3, 9, 4, 5, 6, 7, 8, 10):
        wt, wtb = wr.next()
        cx.load(wt[:], wtb, wv[:, :, i * 512:(i + 1) * 512], q="pool")
        wpre[i] = (wt, wtb)

    def load_w(i):
        return wpre[i]

    st16 = P.ring("st16", 2, [128, T], BF16)
    st32 = P.ring("st32", 2, [128, T], F32)
    t1r = P.ring("t1", 2, [128, 512], F32)
    t2r = P.ring("t2", 2, [128, 512], F32)

    def fm_group(wt, wtb, c4, tq):
        ps, psb = cx.ps.next()
        cx.mm(ps[:], psb, [(wt[:, d, c4 * 128:(c4 + 1) * 128], xT[:, d, tq * 512:(tq + 1) * 512]) for d in range(8)],
              [wtb, xTb])
        return ps, psb

    cx.last_stores = []
    for (i0, out_d) in ((0, qaT_d), (2, kaT_d)):
        wa, wab = load_w(i0)
        wp, wpb = load_w(i0 + 1)
        for c4 in range(4):
            sg, sgb = st16.next()
            for tq in range(4):
                pa, pab = fm_group(wa, wab, c4, tq)
                pp, ppb = fm_group(wp, wpb, c4, tq)
                t1, t1b = t1r.next()
                t2, t2b = t2r.next()
                sl = slice(tq * 512, (tq + 1) * 512)
                P.add("dve", lambda e, t1=t1, pa=pa, sl=sl: e.tensor_tensor(out=t1[:], in0=pa[:], in1=cosT[:, sl], op=ALU.mult),
                      reads=[pab, cosb], writes=[t1b])
                P.add("dve", lambda e, t2=t2, pp=pp, sl=sl: e.tensor_tensor(out=t2[:], in0=pp[:], in1=sinT[:, sl], op=ALU.mult),
                      reads=[ppb, sinb], writes=[t2b])
                P.add("pool", lambda e, t1=t1, t2=t2, sg=sg, sl=sl: e.tensor_tensor(out=sg[:, sl], in0=t1[:], in1=t2[:], op=ALU.add),
                      reads=[t1b, t2b], writes=[sgb])
            cx.store(out_d[c4 * 128:(c4 + 1) * 128, :], sg[:], sgb)
    cx.last_stores_A = list(cx.last_stores)
    vst = P.ring("vst", 3, [128, 512], BF16)

    def tm_block(i0, out_d):
        wa, wab = load_w(i0)
        for tt in range(NT):
            ps, psb = cx.ps.next()
            cx.mm(ps[:], psb, [(xT[:, d, tt * 128:(tt + 1) * 128], wa[:, d, :]) for d in range(8)], [wab, xTb])
            v, vb_ = vst.next()
            P.add("act", lambda e, ps=ps, v=v: e.copy(out=v[:], in_=ps[:]), reads=[psb], writes=[vb_])
            cx.store(out_d(tt), v[:].rearrange("p (j c) -> p j c", j=4), vb_)

    cx.last_stores = []
    tm_block(9, va_d)
    if "setA_hook" in cx.bind:
        cx.bind["setA_hook"](cx.last_stores_A + cx.last_stores)
    for (i0, out_d) in ((4, qbT_d), (5, kbT_d)):
        wa, wab = load_w(i0)
        for c4 in range(4):
            sg, sgb = st16.next()
            for tq in range(4):
                pa, pab = fm_group(wa, wab, c4, tq)
                sl = slice(tq * 512, (tq + 1) * 512)
                P.add("act", lambda e, pa=pa, sg=sg, sl=sl: e.copy(out=sg[:, sl], in_=pa[:]), reads=[pab], writes=[sgb])
            cx.store(out_d[c4 * 128:(c4 + 1) * 128, :], sg[:], sgb)
    wa, wab = load_w(6)
    for c4 in range(4):
        sg, sgb = st32.next()
        for tq in range(4):
            pa, pab = fm_group(wa, wab, c4, tq)
            sl = slice(tq * 512, (tq + 1) * 512)
            P.add("act", lambda e, pa=pa, sg=sg, sl=sl: e.copy(out=sg[:, sl], in_=pa[:]), reads=[pab], writes=[sgb])
        cx.store(gbT_d[c4 * 128:(c4 + 1) * 128, :], sg[:], sgb)
    e1s, e1sb = P.tile("e1s", [128, 4, 2], F32)
    wa, wab = load_w(7)
    wp, wpb = load_w(8)
    for c4 in range(4):
        sg, sgb = st32.next()
        for tq in range(4):
            pa, pab = fm_group(wa, wab, c4, tq)
            pp, ppb = fm_group(wp, wpb, c4, tq)
            t1, t1b = t1r.next()
            sl = slice(tq * 512, (tq + 1) * 512)
            P.add("act", lambda e, pa=pa, t1=t1: e.copy(out=t1[:], in_=pa[:]), reads=[pab], writes=[t1b])
            P.add("dve", lambda e, pp=pp, t1=t1, sg=sg, sl=sl: e.tensor_tensor(out=sg[:, sl], in0=pp[:], in1=t1[:], op=ALU.mult),
                  reads=[ppb, t1b], writes=[sgb])
        cx.store(pT_d[c4 * 128:(c4 + 1) * 128, :], sg[:], sgb)
        P.add("act", lambda e, sg=sg, c4=c4: e.copy(out=e1s[:, c4, 0:1], in_=sg[:, 0:1]), reads=[sgb], writes=[e1sb])
        P.add("act", lambda e, sg=sg, c4=c4: e.copy(out=e1s[:, c4, 1:2], in_=sg[:, T - 1:T]), reads=[sgb, e1sb], writes=[e1sb])
    cx.store(cx.bind["E1src"], e1s[:].rearrange("p a b -> p (a b)"), e1sb)
    tm_block(10, vb_d)


def _perm64(n):
    idx = np.arange(n).reshape(-1, 64)
    return np.concatenate([idx[:, 32:], idx[:, :32]], axis=1).reshape(-1)


def rope_tables(tb):
    half = 32
    inv = (np.float32(10000.0) ** (-np.arange(half, dtype=np.float32) * np.float32(2.0) / np.float32(64))).astype(np.float32)
    pos = np.arange(tb * T, (tb + 1) * T, dtype=np.float32)
    ang = (pos[None, :] * inv[:, None]).astype(np.float32)
    c = np.cos(ang).astype(np.float32)
    s = np.sin(ang).astype(np.float32)
    cosT = np.concatenate([c, c, c, c], axis=0)
    sinT = np.concatenate([-s, s, -s, s], axis=0)
    return np.ascontiguousarray(cosT), np.ascontiguousarray(sinT)


def p1_weight(w):
    sl = lambda i: w[:, i * 512:(i + 1) * 512]
    qa, ka, va, qb, kb, vb, gb, gc, hc = [sl(i) for i in range(9)]
    pm = _perm64(512)
    return np.ascontiguousarray(np.concatenate([qa, qa[:, pm], ka, ka[:, pm], qb, kb, gb, gc, hc, va, vb], axis=1))


def p1_inputs(inp, l, xfull, first):
    w = p1_weight(np.asarray(inp["w_in"][l]))
    maps = []
    for c in range(8):
        b, tb = c // 4, c % 4
        cosT, sinT = rope_tables(tb)
        m = {"x": np.ascontiguousarray(xfull[b, tb * T:(tb + 1) * T]), "w": w, "cosT": cosT, "sinT": sinT}
        if first:
            m["g"] = np.ascontiguousarray(np.broadcast_to(np.asarray(inp["emb_ln_g"])[None, :], (128, D)))
            m["b"] = np.ascontiguousarray(np.broadcast_to(np.asarray(inp["emb_ln_b"])[None, :], (128, D)))
        maps.append(m)
    return maps


def build_p2(cx, lam_init, n_qb=S // 512, n_rows=128):
    P = cx.P
    tab_d = cx.dram_in("tab", [128, 2 * 14 * 64], F32)
    lamv_d = cx.dram_in("lamv", [128, 4 * 64], F32)
    gs_d = cx.dram_in("gs", [128, 1], F32)
    g2s = cx.bind["G2src"].rearrange("(t w p) c -> t w p c", t=4, w=2)
    banks = cx.ps.items
    s_ring = Ring(banks[0:3])
    o_ring = Ring(banks[3:5])
    z_ring = Ring(banks[5:7])
    m_bank = banks[7]

    qT, qTb = P.tile("qT", [128, S], BF16)
    kT, kTb = P.tile("kT", [128, S], BF16)
    v, vb_ = P.tile("v", [128, 64, 128], BF16)
    nqT, nqTb = P.tile("nqT", [128, S], BF16)
    nkT, nkTb = P.tile("nkT", [128, S], BF16)
    nve, nveb = P.tile("nve", [128, 64, 128], BF16)
    nvo, nvob = P.tile("nvo", [128, 63, 128], BF16)
    tab, tabb = P.tile("tab", [128, 2, 14, 64], F32)
    m1 = cx.bind["M1"].rearrange("(w t p) c -> t w p c", t=4, w=6)
    rA = [cx.bind["m1A"]]
    for t in range(4):
        ts_ = slice(t * T, (t + 1) * T)
        cx.load(qT[:, ts_], qTb, m1[t, 0], reads=rA)
        cx.load(kT[:, ts_], kTb, m1[t, 1], reads=rA)
        cx.load(v[:, t * 16:(t + 1) * 16, :].rearrange("p a b -> p (a b)"), vb_, m1[t, 2], reads=rA)

    def na_loads():
        cx.bind["m1b_hook"]()
        rB = [cx.bind["m1B"]]
        for t in range(4):
            ts_ = slice(t * T, (t + 1) * T)
            cx.load(nqT[:, ts_], nqTb, m1[t, 3], reads=rB)
            cx.load(nkT[:, ts_], nkTb, m1[t, 4], reads=rB)
            cx.load(nve[:, t * 16:(t + 1) * 16, :].rearrange("p a b -> p (a b)"), nveb, m1[t, 5], reads=rB)
            n0 = 15 if t == 3 else 16
            cx.load(nvo[0:64, t * 16:t * 16 + n0, :].rearrange("p a b -> p (a b)"), nvob, m1[t, 5, 64:128, 0:n0 * 128], reads=rB)
            a0 = t * 16 - 1 if t > 0 else 0
            c0 = 0 if t > 0 else 128
            cx.load(nvo[64:128, a0:t * 16 + 15, :].rearrange("p a b -> p (a b)"), nvob, m1[t, 5, 0:64, c0:T], reads=rB)

    cx.load(tab[:], tabb, tab_d.rearrange("p (h s c) -> p h s c", h=2, s=14))
    lamv, lamvb = P.tile("lamv", [128, 4, 64], F32)
    gs, gsb = P.tile("gs", [128, 1], F32)
    cx.load(lamv[:], lamvb, lamv_d.rearrange("p (a b) -> p a b", b=64))
    cx.load(gs[:], gsb, gs_d)
    sc, scb = P.tile("sc", [128, 8], F32)
    lp, lpb = P.tile("lp", [128, 2, 64], F32)
    P.add("dve", lambda e: e.tensor_tensor(out=lp[:, 0, :], in0=lamv[:, 0, :], in1=lamv[:, 1, :], op=ALU.mult),
          reads=[lamvb], writes=[lpb])
    P.add("dve", lambda e: e.tensor_tensor(out=lp[:, 1, :], in0=lamv[:, 2, :], in1=lamv[:, 3, :], op=ALU.mult),
          reads=[lamvb, lpb], writes=[lpb])
    P.add("dve", lambda e: e.reduce_sum(out=sc[:, 0:1], in_=lp[:, 0, :], axis=mybir.AxisListType.X), reads=[lpb], writes=[scb])
    P.add("dve", lambda e: e.reduce_sum(out=sc[:, 1:2], in_=lp[:, 1, :], axis=mybir.AxisListType.X), reads=[lpb, scb], writes=[scb])
    P.add("act", lambda e: e.activation(out=sc[:, 2:4], in_=sc[:, 0:2], func=AF.Exp), reads=[scb], writes=[scb])
    P.add("dve", lambda e: e.tensor_tensor(out=sc[:, 4:5], in0=sc[:, 3:4], in1=sc[:, 2:3], op=ALU.subtract), reads=[scb], writes=[scb])
    P.add("dve", lambda e: e.tensor_scalar(out=sc[:, 4:5], in0=sc[:, 4:5], scalar1=-float(lam_init), scalar2=None, op0=ALU.add),
          reads=[scb], writes=[scb])
    P.add("dve", lambda e: e.tensor_scalar(out=sc[:, 5:6], in0=gs[:, 0:1], scalar1=float(1.0 - lam_init), scalar2=None, op0=ALU.mult),
          reads=[scb, gsb], writes=[scb])
    ones32, ones32b = P.tile("ones32", [128, 128], F32)
    P.add("dve", lambda e: e.memset(ones32[:], 1.0 / 128.0), writes=[ones32b])

    pT_ring = P.ring("pT", 6, [128, 512], BF16)
    rz_ring = P.ring("rz", 2, [128, 512], F32)
    om_ring = P.ring("om", 4, [128, 512], F32)
    oa_ring = P.ring("oa", 2, [128, 512], F32)
    sq_ring = P.ring("sq", 2, [128, 512], F32)
    ya_ring = P.ring("ya", 2, [128, 512], BF16)

    NKC = S // 128
    cx.last_stores = []
    ones32u, ones32ub = P.tile("ones32u", [128, 128], F32)
    P.add("dve", lambda e: e.memset(ones32u[:], 1.0), writes=[ones32ub])
    dq = []
    s4_ring = Ring(banks[0:4])
    O_banks = [banks[4], banks[5]]
    z_bank = banks[6]
    oc_ring = P.ring("ocp", 4, [128, 512], F32)
    za_ring2 = P.ring("zacc2", 4, [128, 512], F32)
    for qb in range(n_qb):
        qs = slice(qb * 512, (qb + 1) * 512)
        zas = [[za_ring2.next()] for m in range(2)]

        def qk(kc):
            out = []
            for m in range(2):
                rows = slice(m * 64, (m + 1) * 64)
                s_, sb_ = s4_ring.next()
                cx.mm(s_[:], sb_, [(kT[rows, kc * 128:(kc + 1) * 128], qT[rows, qs])], [kTb, qTb])
                out.append((s_, sb_))
            return out

        cur = qk(0)
        for kc in range(NKC):
            if dq and kc % 4 == 3:
                dq.pop(0)()
            nxt = qk(kc + 1) if kc + 1 < NKC else None
            pts = []
            for m in range(2):
                s_, sb_ = cur[m]
                pt, ptb = pT_ring.next()
                P.add("act", lambda e, s_=s_, pt=pt: e.activation(out=pt[:], in_=s_[:], func=AF.Exp, scale=0.125),
                      reads=[sb_], writes=[ptb])
                pts.append((pt, ptb))
            for m in range(2):
                pt, ptb = pts[m]
                O, Ob = O_banks[m]
                P.add("pe", lambda e, kc=kc, pt=pt, O=O: e.matmul(O[:], lhsT=v[:, kc, :], rhs=pt[:], start=(kc == 0), stop=(kc == NKC - 1)),
                      reads=[vb_, ptb], writes=[Ob])
                za, zab = zas[m][0]
                if kc < 1:
                    P.add("dve", lambda e, za=za, pt=pt: e.tensor_copy(out=za[:], in_=pt[:]), reads=[ptb], writes=[zab])
                else:
                    P.add("dve", lambda e, za=za, pt=pt: e.tensor_tensor(out=za[:], in0=za[:], in1=pt[:], op=ALU.add),
                          reads=[ptb, zab], writes=[zab])
            cur = nxt
        ocs = []
        for m in range(2):
            oc, ocb = oc_ring.next()
            O, Ob = O_banks[m]
            P.add("act", lambda e, oc=oc, O=O: e.copy(out=oc[:], in_=O[:]), reads=[Ob], writes=[ocb])
            ocs.append((oc, ocb))
        oms = [om_ring.next() for m in range(2)]

        def mk_stage0(m, zas=zas, ocs=ocs, oms=oms):
            def stage0():
                Z, Zb = z_bank
                cx.mm(Z[:], Zb, [(ones32u[:], z_[0][:]) for z_ in zas[m]], [ones32ub] + [z_[1] for z_ in zas[m]])
                rz, rzb = rz_ring.next()
                P.add("dve", lambda e: e.reciprocal(out=rz[:], in_=Z[:]), reads=[Zb], writes=[rzb])
                oc, ocb = ocs[m]
                om, omb = oms[m]
                P.add("dve", lambda e: e.tensor_tensor(out=om[:], in0=oc[:], in1=rz[:], op=ALU.mult),
                      reads=[ocb, rzb], writes=[omb])
            return stage0

        oa, oab = oa_ring.next()
        sq, sqb = sq_ring.next()

        def stage0b(oms=oms, oa=oa, oab=oab):
            (o0, o0b), (o1, o1b) = oms
            P.add("dve", lambda e: e.scalar_tensor_tensor(out=oa[:], in0=o1[:], scalar=sc[:, 4:5], in1=o0[:],
                                                          op0=ALU.mult, op1=ALU.add),
                  reads=[o0b, o1b, scb], writes=[oab])

        def stage1(oa=oa, oab=oab, sq=sq, sqb=sqb):
            P.add("act", lambda e: e.activation(out=sq[:], in_=oa[:], func=AF.Square), reads=[oab], writes=[sqb])
            ms, msb = m_bank
            cx.mm(ms[:], msb, [(ones32[:], sq[:])], [ones32b, sqb])

        def stage2(oa=oa, oab=oab, sq=sq, sqb=sqb, qb=qb):
            ms, msb = m_bank
            P.add("act", lambda e: e.activation(out=sq[:], in_=ms[:], func=AF.Sqrt, bias=EPS, scale=1.0),
                  reads=[msb], writes=[sqb])
            P.add("dve", lambda e: e.reciprocal(out=sq[:], in_=sq[:]), reads=[sqb], writes=[sqb])
            ya, yab = ya_ring.next()
            P.add("dve", lambda e: e.scalar_tensor_tensor(out=ya[:], in0=oa[:], scalar=sc[:, 5:6], in1=sq[:],
                                                          op0=ALU.mult, op1=ALU.mult),
                  reads=[oab, sqb, scb], writes=[yab])
            cx.store(g2s[qb // 4, 0, :, (qb % 4) * 512:(qb % 4 + 1) * 512], ya[:], yab)

        dq.extend([mk_stage0(0), mk_stage0(1), stage0b, stage1, stage2])
    while dq:
        dq.pop(0)()

    if "ya_hook" in cx.bind:
        cx.bind["ya_hook"](list(cx.last_stores))
    na_loads()
    ybT = [P.tile(f"ybT{hh}", [64, S], BF16) for hh in range(2)]
    t_ring = P.ring("nt", 3, [128, 256], F32)
    np_ring = P.ring("npT", 3, [128, 256], BF16)
    nrz_ring = P.ring("nrz", 3, [64, 64], F32)
    pz_ring = Ring(banks[3:7])
    items = [(r, hh) for r in range(n_rows) for hh in range(2)]

    def nqk(r, hh):
        rs = min(max(r - 4, 0), 120)
        s, sb = s_ring.next()
        rows = slice(hh * 64, (hh + 1) * 64)

        def fn(e):
            ins = None
            for jj in range(4):
                k0 = (rs + 2 * jj) * 64
                ins = e.matmul(s[:, jj * 64:(jj + 1) * 64], lhsT=nkT[rows, k0:k0 + 128], rhs=nqT[rows, r * 64:(r + 1) * 64],
                               start=True, stop=True)
            return ins

        P.add("pe", fn, reads=[nkTb, nqTb], writes=[sb])
        return s, sb

    cur = nqk(*items[0]) if items else None
    for ii, (r, hh) in enumerate(items):
        nxt = nqk(*items[ii + 1]) if ii + 1 < len(items) else None
        rs = min(max(r - 4, 0), 120)
        vv = rs - r + 7
        s, sb = cur
        t, tb_ = t_ring.next()
        P.add("dve", lambda e, t=t, s=s, hh=hh, vv=vv: e.scalar_tensor_tensor(
            out=t[:].rearrange("p (a b) -> p a b", b=64), in0=s[:, 0:256].rearrange("p (a b) -> p a b", b=64), scalar=0.125,
            in1=tab[:, hh, vv:vv + 7:2, :], op0=ALU.mult, op1=ALU.add), reads=[sb, tabb], writes=[tb_])
        pt, ptb = np_ring.next()
        P.add("act", lambda e, t=t, pt=pt: e.activation(out=pt[:], in_=t[:], func=AF.Exp), reads=[tb_], writes=[ptb])
        pz, pzb = pz_ring.next()

        def fn(e, rs=rs, hh=hh, pt=pt, pz=pz):
            for jj in range(4):
                kr = rs + 2 * jj
                vt = nve[:, kr // 2, hh * 64:(hh + 1) * 64] if kr % 2 == 0 else nvo[:, (kr - 1) // 2, hh * 64:(hh + 1) * 64]
                e.matmul(pz[0:64, 0:64], lhsT=vt, rhs=pt[:, jj * 64:(jj + 1) * 64], start=(jj == 0), stop=(jj == 3))
            ins = None
            for jj in range(4):
                ins = e.matmul(pz[0:64, 64:128], lhsT=cx.ones[:, 0:64], rhs=pt[:, jj * 64:(jj + 1) * 64], start=(jj == 0), stop=(jj == 3))
            return ins

        P.add("pe", fn, reads=[nveb, nvob, ptb, cx.onesb], writes=[pzb])
        rz, rzb = nrz_ring.next()
        P.add("dve", lambda e, rz=rz, pz=pz: e.reciprocal(out=rz[:], in_=pz[0:64, 64:128]), reads=[pzb], writes=[rzb])
        yt, ytb = ybT[hh]
        P.add("dve", lambda e, yt=yt, pz=pz, rz=rz, r=r: e.tensor_tensor(out=yt[:, r * 64:(r + 1) * 64], in0=pz[0:64, 0:64], in1=rz[:], op=ALU.mult),
              reads=[pzb, rzb], writes=[ytb])
        cur = nxt
    for hh in range(2):
        yt, ytb = ybT[hh]
        for tb_ in range(4):
            cx.store(g2s[tb_, 1, hh * 64:(hh + 1) * 64, :], yt[:, tb_ * T:(tb_ + 1) * T], ytb)


def na_table(rpb_l, heads):
    p = np.arange(128)
    half = p // 64
    kc = p % 64
    c = np.arange(64)
    cs = np.clip(c - 8, 0, 48)
    valid = (kc[:, None] >= cs[None, :]) & (kc[:, None] < cs[None, :] + 16)
    dc = np.clip(kc[:, None] - c[None, :] + 15, 0, 30)
    tab = np.empty((128, 2, 14, 64), np.float32)
    for hi, h in enumerate(heads):
        for s_ in range(14):
            dr = s_ + half
            g = rpb_l[h][dr[:, None], dc]
            tab[:, hi, s_, :] = np.where(valid, g, np.float32(NEG))
    return tab.reshape(128, -1)


def p2_inputs(inp, l, r1):
    maps = []
    lamv = np.stack([np.asarray(inp[k][l]) for k in ("lam_q1", "lam_k1", "lam_q2", "lam_k2")], 0).reshape(1, -1)
    lamv = np.ascontiguousarray(np.broadcast_to(lamv, (128, 256))).astype(np.float32)
    gs = np.ascontiguousarray(np.asarray(inp["subln_g"][l]).reshape(128, 1)).astype(np.float32)
    rpb_l = np.asarray(inp["rpb"][l])
    for c in range(8):
        b, j = c // 4, c % 4
        rows = slice(j * 128, (j + 1) * 128)
        cat = lambda key: np.ascontiguousarray(np.concatenate([r1[b * 4 + t][key][rows] for t in range(4)], axis=1))
        catv = lambda key: np.concatenate([r1[b * 4 + t][key][:, rows] for t in range(4)], axis=0)
        v = catv("va")
        nv = catv("vb")
        tok = lambda a: np.ascontiguousarray(a.reshape(-1, 128, 128).transpose(1, 0, 2).reshape(128, -1))
        maps.append({
            "q": cat("qaT"), "k": cat("kaT"), "v": tok(v),
            "nq": cat("qbT"), "nk": cat("kbT"), "nve": tok(nv), "nvo": tok(nv[64:64 + 63 * 128]),
            "tab": na_table(rpb_l, (2 * j, 2 * j + 1)), "lamv": lamv, "gs": gs,
        })
    return maps


def residual_ln_out(cx, mm_pairs_fn, reads, x_d, g, gb_, b, bb_, tmp, rings, out_d, xT, xTb, tt, tt_out=None):
    P = cx.P
    xt, xtb = rings["x"].next()
    cx.load(xt[:], xtb, x_d[tt * 128:(tt + 1) * 128, :])
    r, rb = rings["r"].next()
    for half in range(2):
        ps, psb = cx.ps.next()
        cx.mm(ps[:], psb, mm_pairs_fn(half), reads)
        hs = slice(half * 512, (half + 1) * 512)
        P.add("dve", lambda e, r=r, xt=xt, ps=ps, hs=hs: e.scalar_tensor_tensor(
            out=r[:, hs], in0=xt[:, hs], scalar=float(ALPHA), in1=ps[:], op0=ALU.mult, op1=ALU.add),
            reads=[xtb, psb], writes=[rb])
    o, ob = rings["o"].next()
    layer_norm_tile(cx, r, rb, g, gb_, b, bb_, o, ob, tmp)
    cx.store(out_d[tt * 128:(tt + 1) * 128, :], o[:], ob)
    if xT is not None:
        xb16, xb16b = rings["xbf"].next()
        P.add("dve", lambda e, o=o, xb16=xb16: e.tensor_copy(out=xb16[:], in_=o[:]), reads=[ob], writes=[xb16b])
        transpose_tile(cx, xb16, xb16b, xT, xTb, tt if tt_out is None else tt_out)


def std_rings(P):
    return {"x": P.ring("rx", 2, [128, D], F32), "r": P.ring("rr", 1, [128, D], F32),
            "o": P.ring("ro", 2, [128, D], F32), "xbf": P.ring("rxbf", 2, [128, D], BF16)}


def build_p3a(cx):
    P = cx.P
    x_d = cx.dram_in("x", [T, D], F32)
    xT_d = cx.dram_in("xT", [D, T], BF16)
    p_d = cx.dram_in("pTh", [512, T + 2], F32)
    gbT_d = cx.dram_in("gbT", [512, T], F32)
    cw_d = cx.dram_in("cw", [128, 12], F32)
    wg_d = cx.dram_in("wg", [D, 3072], F32)
    wb_d = cx.dram_in("wb", [3, 512, D], F32)
    wm_d = cx.dram_in("wm", [D, D], F32)
    g_d = cx.dram_in("g", [128, D], F32)
    b_d = cx.dram_in("b", [128, D], F32)
    x1_d = cx.dram_out("x1", [T, D], F32)
    x1T_d = cx.dram_out("x1T", [D, T], BF16)
    xT, xTb = P.tile("xT", [128, 8, T], BF16)
    cx.load(xT[:], xTb, xT_d.rearrange("(dc p) t -> p dc t", p=128))
    yT, yTb = P.tile("yT", [128, 12, T], BF16)
    yab, ybb, ycb = Buf(), Buf(), Buf()
    m2 = cx.bind["M2"].rearrange("(w j p) c -> j w p c", j=4, w=2)
    for w_, bb2 in ((0, yab), (1, ybb)):
        for j_ in range(4):
            cx.load(yT[:, w_ * 4 + j_, :], bb2, m2[j_, w_])
    cw, cwb = P.tile("cw", [128, 4, 3], F32)
    cx.load(cw[:], cwb, cw_d.rearrange("p (a b) -> p a b", b=3))
    g, gb_ = P.tile("g_s", [128, D], F32)
    b, bb_ = P.tile("b_s", [128, D], F32)
    cx.load(g[:], gb_, g_d)
    cx.load(b[:], bb_, b_d)
    pr = P.ring("pp", 2, [128, 514], F32)
    gr = P.ring("gq", 2, [128, 512], F32)
    ar = P.ring("acc", 2, [128, 512], F32)
    for c4 in range(4):
        rows = slice(c4 * 128, (c4 + 1) * 128)
        for qt in range(4):
            pt, ptb = pr.next()
            gt, gtb = gr.next()
            ac, acb = ar.next()
            cx.load(pt[:], ptb, p_d[rows, qt * 512:qt * 512 + 514])
            cx.load(gt[:], gtb, gbT_d[rows, qt * 512:(qt + 1) * 512])
            P.add("dve", lambda e, ac=ac, pt=pt, c4=c4: e.tensor_scalar(out=ac[:], in0=pt[:, 0:512], scalar1=cw[:, c4, 0:1], scalar2=None,
                                                                        op0=ALU.mult), reads=[ptb, cwb], writes=[acb])
            for k in (1, 2):
                P.add("dve", lambda e, ac=ac, pt=pt, c4=c4, k=k: e.scalar_tensor_tensor(
                    out=ac[:], in0=pt[:, k:k + 512], scalar=cw[:, c4, k:k + 1], in1=ac[:], op0=ALU.mult, op1=ALU.add),
                    reads=[ptb, cwb, acb], writes=[acb])
            P.add("pool", lambda e, ac=ac, gt=gt, c4=c4, qt=qt: e.tensor_tensor(
                out=yT[:, 8 + c4, qt * 512:(qt + 1) * 512], in0=ac[:], in1=gt[:], op=ALU.mult), reads=[acb, gtb], writes=[ycb])
    ybufs = [yab, ybb, ycb]
    mT, mTb = P.tile("mT", [128, 8, T], BF16)
    wgr = P.ring("wgc", 2, [128, 8, 3, 128], BF16)
    wbr = P.ring("wbc", 2, [128, 4, 3, 128], BF16)
    sgr = P.ring("sig", 3, [128, 512], F32)
    mar = P.ring("macc", 2, [128, 512], F32)
    tmr = P.ring("mtmp", 2, [128, 512], F32)
    wgv = wg_d.rearrange("(dc p) (br n) -> p dc br n", p=128, br=3)
    wbv = wb_d.rearrange("br (c p) n -> p c br n", p=128)
    for cc in range(8):
        cs = slice(cc * 128, (cc + 1) * 128)
        wgt, wgtb = wgr.next()
        wbt, wbtb = wbr.next()
        for br in range(3):
            cx.load(wgt[:, :, br, :], wgtb, wgv[:, :, br, cs], q="pool")
            cx.load(wbt[:, :, br, :], wbtb, wbv[:, :, br, cs], q="pool")
        for tq in range(4):
            ts = slice(tq * 512, (tq + 1) * 512)
            ma, mab = mar.next()
            for br in range(3):
                G, Gb = cx.ps.next()
                cx.mm(G[:], Gb, [(wgt[:, d, br, :], xT[:, d, ts]) for d in range(8)], [wgtb, xTb])
                sg, sgb = sgr.next()
                P.add("act", lambda e, sg=sg, G=G: e.activation(out=sg[:], in_=G[:], func=AF.Sigmoid), reads=[Gb], writes=[sgb])
                B, Bb = cx.ps.next()
                cx.mm(B[:], Bb, [(wbt[:, c4, br, :], yT[:, br * 4 + c4, ts]) for c4 in range(4)], [wbtb, ybufs[br]])
                if br == 0:
                    P.add("dve", lambda e, ma=ma, B=B, sg=sg: e.tensor_tensor(out=ma[:], in0=B[:], in1=sg[:], op=ALU.mult),
                          reads=[Bb, sgb], writes=[mab])
                else:
                    tm, tmb = tmr.next()
                    P.add("dve", lambda e, tm=tm, B=B, sg=sg: e.tensor_tensor(out=tm[:], in0=B[:], in1=sg[:], op=ALU.mult),
                          reads=[Bb, sgb], writes=[tmb])
                    if br == 1:
                        P.add("pool", lambda e, ma=ma, tm=tm: e.tensor_tensor(out=ma[:], in0=ma[:], in1=tm[:], op=ALU.add),
                              reads=[mab, tmb], writes=[mab])
                    else:
                        P.add("pool", lambda e, ma=ma, tm=tm, cc=cc, ts=ts: e.tensor_tensor(out=mT[:, cc, ts], in0=ma[:], in1=tm[:], op=ALU.add),
                              reads=[mab, tmb], writes=[mTb])
    wr = P.ring("wblk", 2, [128, 8, 512], BF16)
    wmv = wview(wm_d)
    wh = []
    for half in range(2):
        wt, wtb = wr.next()
        cx.load(wt[:], wtb, wmv[:, :, half * 512:(half + 1) * 512], q="pool")
        wh.append((wt, wtb))
    tmp = ln_tmp(P)
    rings = std_rings(P)
    for tt in range(NT):
        tsl = slice(tt * 128, (tt + 1) * 128)
        residual_ln_out(cx, lambda half, tsl=tsl: [(mT[:, c8, tsl], wh[half][0][:, c8, :]) for c8 in range(8)],
                        [mTb, wh[0][1], wh[1][1]], x_d, g, gb_, b, bb_, tmp, rings, x1_d, xT, xTb, tt)
    cx.store(x1T_d.rearrange("(dc p) t -> p dc t", p=128), xT[:], xTb)


def build_p3b(cx):
    P = cx.P
    x_d = cx.dram_in("x1", [T, D], F32)
    xT_d = cx.dram_in("x1T", [D, T], BF16)
    mem_d = cx.dram_in("mem", [256, D], F32)
    wq_d = cx.dram_in("wq", [D, D], F32)
    wkv_d = cx.dram_in("wkv", [D, 2 * D], F32)
    wo_d = cx.dram_in("wo", [D, D], F32)
    g_d = cx.dram_in("g", [128, D], F32)
    b_d = cx.dram_in("b", [128, D], F32)
    x2_d = cx.dram_out("x2", [T, D], F32)
    x2T_d = cx.dram_out("x2T", [D, T], BF16)
    xT, xTb = P.tile("xT", [128, 8, T], BF16)
    cx.load(xT[:], xTb, xT_d.rearrange("(dc p) t -> p dc t", p=128))
    g, gb_ = P.tile("g_s", [128, D], F32)
    b, bb_ = P.tile("b_s", [128, D], F32)
    cx.load(g[:], gb_, g_d)
    cx.load(b[:], bb_, b_d)
    mem32, mem32b = P.tile("mem32", [128, 2, D], F32)
    cx.load(mem32[:], mem32b, mem_d.rearrange("(mt p) d -> p mt d", p=128))
    mem16, mem16b = P.tile("mem16", [128, 2, D], BF16)
    P.add("dve", lambda e: e.tensor_copy(out=mem16[:], in_=mem32[:]), reads=[mem32b], writes=[mem16b])
    memT, memTb = P.tile("memT", [128, 8, 256], BF16)
    for mt in range(2):
        for half in range(2):
            ps, psb = cx.ps.next()

            def fn(e, ps=ps, half=half, mt=mt):
                ins = None
                for j in range(4):
                    d = half * 4 + j
                    ins = e.matmul(ps[:, j * 128:(j + 1) * 128], lhsT=mem16[:, mt, d * 128:(d + 1) * 128], rhs=cx.ident[:],
                                   start=True, stop=True)
                return ins

            P.add("pe", fn, reads=[mem16b, cx.identb], writes=[psb])
            P.add("act", lambda e, ps=ps, half=half, mt=mt: e.copy(
                out=memT[:, half * 4:half * 4 + 4, mt * 128:(mt + 1) * 128],
                in_=ps[:, 0:512].rearrange("p (j t) -> p j t", j=4)), reads=[psb], writes=[memTb])
    wr = P.ring("wblk", 3, [128, 8, 512], BF16)

    def load_w(w_d, i):
        wt, wtb = wr.next()
        cx.load(wt[:], wtb, wview(w_d)[:, :, i * 512:(i + 1) * 512], q="pool")
        return wt, wtb

    KxT, KxTb = P.tile("KxT", [128, 8, 256], BF16)
    Vx, Vxb = P.tile("Vx", [128, 2, D], BF16)
    for blk in range(2):
        wt, wtb = load_w(wkv_d, blk)
        for c4 in range(4):
            ps, psb = cx.ps.next()
            cx.mm(ps[:, 0:256], psb, [(wt[:, d, c4 * 128:(c4 + 1) * 128], memT[:, d, :]) for d in range(8)], [wtb, memTb])
            P.add("act", lambda e, ps=ps, blk=blk, c4=c4: e.copy(out=KxT[:, blk * 4 + c4, :], in_=ps[:, 0:256]), reads=[psb], writes=[KxTb])
    for blk in range(2):
        wt, wtb = load_w(wkv_d, 2 + blk)
        for mt in range(2):
            ps, psb = cx.ps.next()
            cx.mm(ps[:], psb, [(memT[:, d, mt * 128:(mt + 1) * 128], wt[:, d, :]) for d in range(8)], [wtb, memTb])
            P.add("act", lambda e, ps=ps, blk=blk, mt=mt: e.copy(out=Vx[:, mt, blk * 512:(blk + 1) * 512], in_=ps[:]), reads=[psb], writes=[Vxb])
    qxT, qxTb = P.tile("qxT", [128, 8, T], BF16)
    for blk in range(2):
        wt, wtb = load_w(wq_d, blk)
        for c4 in range(4):
            for tq in range(4):
                ts = slice(tq * 512, (tq + 1) * 512)
                ps, psb = cx.ps.next()
                cx.mm(ps[:], psb, [(wt[:, d, c4 * 128:(c4 + 1) * 128], xT[:, d, ts]) for d in range(8)], [wtb, xTb])
                P.add("act", lambda e, ps=ps, blk=blk, c4=c4, ts=ts: e.copy(out=qxT[:, blk * 4 + c4, ts], in_=ps[:]), reads=[psb], writes=[qxTb])
    oxT, oxTb = P.tile("oxT", [128, 8, T], BF16)
    ptr = P.ring("xpT", 4, [128, 512], BF16)
    rzr = P.ring("xrz", 2, [128, 512], F32)
    for hx in range(4):
        for tq in range(4):
            ts = slice(tq * 512, (tq + 1) * 512)
            pts = []
            for mt in range(2):
                s, sb = cx.ps.next()
                cx.mm(s[:], sb, [(KxT[:, 2 * hx + dc, mt * 128:(mt + 1) * 128], qxT[:, 2 * hx + dc, ts]) for dc in range(2)], [KxTb, qxTb])
                pt, ptb = ptr.next()
                P.add("act", lambda e, s=s, pt=pt: e.activation(out=pt[:], in_=s[:], func=AF.Exp, scale=1.0 / 16.0), reads=[sb], writes=[ptb])
                pts.append((pt, ptb))
            Z, Zb = cx.ps.next()
            cx.mm(Z[:], Zb, [(cx.ones[:], pts[mt][0][:]) for mt in range(2)], [cx.onesb, pts[0][1], pts[1][1]])
            rz, rzb = rzr.next()
            P.add("dve", lambda e, rz=rz, Z=Z: e.reciprocal(out=rz[:], in_=Z[:]), reads=[Zb], writes=[rzb])
            for dc in range(2):
                O, Ob = cx.ps.next()
                cx.mm(O[:], Ob, [(Vx[:, mt, (2 * hx + dc) * 128:(2 * hx + dc + 1) * 128], pts[mt][0][:]) for mt in range(2)],
                      [Vxb, pts[0][1], pts[1][1]])
                P.add("dve", lambda e, O=O, rz=rz, hx=hx, dc=dc, ts=ts: e.tensor_tensor(out=oxT[:, 2 * hx + dc, ts], in0=O[:], in1=rz[:], op=ALU.mult),
                      reads=[Ob, rzb], writes=[oxTb])
    wh = [load_w(wo_d, half) for half in range(2)]
    tmp = ln_tmp(P)
    rings = std_rings(P)
    for tt in range(NT):
        tsl = slice(tt * 128, (tt + 1) * 128)
        residual_ln_out(cx, lambda half, tsl=tsl: [(oxT[:, c8, tsl], wh[half][0][:, c8, :]) for c8 in range(8)],
                        [oxTb, wh[0][1], wh[1][1]], x_d, g, gb_, b, bb_, tmp, rings, x2_d, xT, xTb, tt)
    cx.store(x2T_d.rearrange("(dc p) t -> p dc t", p=128), xT[:], xTb)
    e3s, e3sb = P.tile("e3s", [128, 8, 2], F32)
    P.add("act", lambda e: e.copy(out=e3s[:, :, 0:1], in_=xT[:, :, 0:1]), reads=[xTb], writes=[e3sb])
    P.add("act", lambda e: e.copy(out=e3s[:, :, 1:2], in_=xT[:, :, T - 1:T]), reads=[xTb, e3sb], writes=[e3sb])
    cx.store(cx.bind["E3src"], e3s[:].rearrange("p a b -> p (a b)"), e3sb)


def build_p4(cx):
    P = cx.P
    TH = T // 2
    x_d = cx.dram_in("x2", [T, D], F32)
    xT_d = cx.dram_in("x2T", [D, T], BF16)
    wi_d = cx.dram_in("wi", [D, 2 * DFF], F32)
    cw_d = cx.dram_in("cwf", [128, NF * 4], F32)
    wo_d = cx.dram_in("wo", [DFF, D], F32)
    g_d = cx.dram_in("g", [128, D], F32)
    b_d = cx.dram_in("b", [128, D], F32)
    x3_d = cx.dram_out("x3", [T, D], F32)
    xT, xTb = P.tile("xT", [128, 8, T + 2], BF16)
    cx.load(xT[:, :, 1:T + 1], xTb, xT_d.rearrange("(dc p) t -> p dc t", p=128))
    halo_fix(cx, cx.bind["E3"], 8, F32, None, *cx.bind["masks"], sb_target=(xT, xTb))
    g, gb_ = P.tile("g_s", [128, D], F32)
    b, bb_ = P.tile("b_s", [128, D], F32)
    cx.load(g[:], gb_, g_d)
    cx.load(b[:], bb_, b_d)
    cw, cwb = P.tile("cwf", [128, NF, 4], F32)
    cx.load(cw[:], cwb, cw_d.rearrange("p (a b) -> p a b", b=4))
    Wo, Wob = P.tile("Wo", [128, NF, D], BF16)
    wov = wo_d.rearrange("(f p) n -> p f n", p=128)
    for i in range(2):
        cx.load(Wo[:, i * 11:(i + 1) * 11, :], Wob, wov[:, i * 11:(i + 1) * 11, :], q="pool")
    hT, hTb = P.tile("hT", [128, NF, TH], BF16)
    wur = P.ring("wu", 2, [128, 8, 128], BF16)
    wgr = P.ring("wgt", 2, [128, 8, 128], BF16)
    gtr = P.ring("gt", 2, [128, TH + 2], F32)
    acr = P.ring("facc", 2, [128, TH], F32)
    slr = P.ring("fsil", 2, [128, TH], F32)
    wiv = wview(wi_d)
    tmp = ln_tmp(P)
    rings = std_rings(P)
    for th in range(2):
        c0 = th * TH
        for f in range(NF):
            wu, wub = wur.next()
            wg, wgb = wgr.next()
            cx.load(wu[:], wub, wiv[:, :, f * 128:(f + 1) * 128], q="pool")
            cx.load(wg[:], wgb, wiv[:, :, DFF + f * 128:DFF + (f + 1) * 128], q="pool")
            U = []
            for tq in range(2):
                ps, psb = cx.ps.next()
                cs = slice(c0 + 1 + tq * 512, c0 + 1 + (tq + 1) * 512)
                cx.mm(ps[:], psb, [(wu[:, d, :], xT[:, d, cs]) for d in range(8)], [wub, xTb])
                U.append((ps, psb))
            gt, gtb = gtr.next()
            for (a0, n) in ((0, 512), (512, 512), (1024, 2)):
                ps, psb = cx.ps.next()
                cx.mm(ps[:, 0:n], psb, [(wg[:, d, :], xT[:, d, c0 + a0:c0 + a0 + n]) for d in range(8)], [wgb, xTb])
                P.add("act", lambda e, ps=ps, gt=gt, a0=a0, n=n: e.copy(out=gt[:, a0:a0 + n], in_=ps[:, 0:n]), reads=[psb], writes=[gtb])
            ac, acb = acr.next()
            P.add("dve", lambda e, ac=ac, gt=gt, f=f: e.tensor_scalar(out=ac[:], in0=gt[:, 0:TH], scalar1=cw[:, f, 0:1], scalar2=cw[:, f, 3:4],
                                                                      op0=ALU.mult, op1=ALU.add), reads=[gtb, cwb], writes=[acb])
            for k in (1, 2):
                P.add("dve", lambda e, ac=ac, gt=gt, f=f, k=k: e.scalar_tensor_tensor(
                    out=ac[:], in0=gt[:, k:k + TH], scalar=cw[:, f, k:k + 1], in1=ac[:], op0=ALU.mult, op1=ALU.add),
                    reads=[gtb, cwb, acb], writes=[acb])
            sl, slb = slr.next()
            P.add("act", lambda e, sl=sl, ac=ac: e.activation(out=sl[:], in_=ac[:], func=AF.Silu), reads=[acb], writes=[slb])
            for tq in range(2):
                ps, psb = U[tq]
                P.add("dve", lambda e, ps=ps, sl=sl, f=f, tq=tq: e.tensor_tensor(
                    out=hT[:, f, tq * 512:(tq + 1) * 512], in0=ps[:], in1=sl[:, tq * 512:(tq + 1) * 512], op=ALU.mult),
                    reads=[psb, slb], writes=[hTb])
        for t8 in range(NT // 2):
            tt = th * (NT // 2) + t8
            tsl = slice(t8 * 128, (t8 + 1) * 128)
            residual_ln_out(cx, lambda half, tsl=tsl: [(hT[:, f, tsl], Wo[:, f, half * 512:(half + 1) * 512]) for f in range(NF)],
                            [hTb, Wob], x_d, g, gb_, b, bb_, tmp, rings, x3_d, None, None, tt)


class FMView:
    def __init__(self, g1s, which):
        self.g1s = g1s
        self.which = which

    def __getitem__(self, idx):
        rows, cols = idx
        j = rows.start // 128
        return self.g1s[j, self.which, :, cols]


def halo_fix(cx, E_ap, n, dtype, target, mL, mLb, mR, mRb, sb_target=None, e_reads=()):
    P = cx.P
    e, eb = P.tile("he", [128, 4, n, 2], dtype)
    cx.load(e[:].rearrange("p t a b -> p t (a b)"), eb, E_ap.rearrange("(t p) x -> p t x", p=128), reads=list(e_reads))
    tv = target.rearrange("(c p) t -> p c t", p=128) if target is not None else None
    for (m, mb, src_col, dst_col, nm) in ((mL, mLb, 1, 0, "hl"), (mR, mRb, 0, T + 1, "hr")):
        acc, accb = P.tile(nm + "a", [128, n, 1], F32)
        out, outb = P.tile(nm + "o", [128, n, 1], dtype)
        P.add("dve", lambda e_, acc=acc, m=m, sc=src_col: e_.tensor_scalar(out=acc[:], in0=e[:, 0, :, sc:sc + 1], scalar1=m[:, 0:1], scalar2=None,
                                                                       op0=ALU.mult), reads=[eb, mb], writes=[accb])
        for s_ in range(1, 4):
            P.add("dve", lambda e_, acc=acc, m=m, sc=src_col, s_=s_: e_.scalar_tensor_tensor(
                out=acc[:], in0=e[:, s_, :, sc:sc + 1], scalar=m[:, s_:s_ + 1], in1=acc[:], op0=ALU.mult, op1=ALU.add),
                reads=[eb, mb, accb], writes=[accb])
        if sb_target is not None:
            tt_, ttb_ = sb_target
            P.add("dve", lambda e_, acc=acc, tt_=tt_, dc=dst_col: e_.tensor_copy(out=tt_[:, :, dc:dc + 1], in_=acc[:]), reads=[accb], writes=[ttb_])
        else:
            P.add("dve", lambda e_, acc=acc, out=out: e_.tensor_copy(out=out[:], in_=acc[:]), reads=[accb], writes=[outb])
            cx.store(tv[:, :, dst_col:dst_col + 1], out[:], outb, slow=True)


def build_fused(stop=99):
    nc = bass.Bass("TRN2", target_bir_lowering=False)
    cx = Ctx(nc)
    P = cx.P
    cx.consts()
    mL_d = cx.dram_in("mL", [128, 4], F32)
    mR_d = cx.dram_in("mR", [128, 4], F32)
    mL, mLb = P.tile("mL", [128, 4], F32)
    mR, mRb = P.tile("mR", [128, 4], F32)
    cx.load(mL[:], mLb, mL_d)
    cx.load(mR[:], mRb, mR_d)
    sc = cx.scratch
    G1src = sc("G1src", [4 * 6 * 128, T], BF16)
    G1 = sc("G1", [4 * 4 * 6 * 128, T], BF16)
    M1 = sc("M1", [4 * 6 * 128, T], BF16)
    M2 = sc("M2", [4 * 2 * 128, T], BF16)
    G2src = sc("G2src", [4 * 2 * 128, T], BF16)
    G2 = sc("G2", [4 * 4 * 2 * 128, T], BF16)
    E1src = sc("E1src", [128, 8], F32)
    E1 = sc("E1", [512, 8], F32)
    E3src = sc("E3src", [128, 16], F32)
    E3 = sc("E3", [512, 16], F32)
    xres = sc("xres", [T, D], F32)
    x3s = sc("x3s", [T, D], F32)
    xTs = sc("xTs", [D, T], BF16)
    pTh = sc("pThs", [512, T + 2], F32)
    gbTs = sc("gbTs", [512, T], F32)
    x1s = sc("x1s", [T, D], F32)
    x1Ts = sc("x1Ts", [D, T], BF16)
    x2s = sc("x2s", [T, D], F32)
    x2Ts = sc("x2Ts", [D, T], BF16)
    out_d = nc.dram_tensor("out", [T, D], F32, kind="ExternalOutput").ap()
    g1s = G1src.rearrange("(j w p) c -> j w p c", j=4, w=6)

    def vview(which):
        return lambda tt: g1s[:, which, :, tt * 128:(tt + 1) * 128].rearrange("j p c -> p j c")

    for l in range(DEPTH):
        lam_init = 0.8 - 0.6 * math.exp(-0.3 * l)
        first = (l == 0)
        xin = xres if first else x3s
        cx.suffix = f"_p1_{l}"
        cx.bind = {"xln": xres, "xT": xTs, "qaT": FMView(g1s, 0), "kaT": FMView(g1s, 1), "qbT": FMView(g1s, 3), "kbT": FMView(g1s, 4),
                   "va": vview(2), "vb": vview(5), "gbT": gbTs, "pT": pTh[:, 1:T + 1], "E1src": E1src}
        if not first:
            cx.bind["x"] = x3s
        gA, gB, e1g, m1A, m1B = Buf(), Buf(), Buf(), Buf(), Buf()

        def setA_hook(stores, gA=gA):
            for w_ in range(3):
                for j_ in range(4):
                    blk = j_ * 6 + w_
                    cx.allgather(G1src[blk * 128:(blk + 1) * 128, :], G1[blk * 512:(blk + 1) * 512, :], writes=[gA], after=stores)

        cx.bind["setA_hook"] = setA_hook
        with cx.phase(f"L{l}p1_"):
            build_p1(cx, first)
        if stop == 1:
            break
        P.barrier()
        cx.allgather(E1src, E1, writes=[e1g])
        for w_ in range(3, 6):
            for j_ in range(4):
                blk = j_ * 6 + w_
                cx.allgather(G1src[blk * 128:(blk + 1) * 128, :], G1[blk * 512:(blk + 1) * 512, :], writes=[gB])
        g1v = G1.rearrange("(j r) c -> j r c", j=4)
        cx.load_dyn(M1[0:1536, :].rearrange("(a r) c -> a r c", a=1), m1A,
                    lambda e: g1v[bass.ds(cx.rank(e), 1), 0:1536, :], reads=[gA])

        def m1b_hook():
            cx.load_dyn(M1[1536:3072, :].rearrange("(a r) c -> a r c", a=1), m1B,
                        lambda e: g1v[bass.ds(cx.rank(e), 1), 1536:3072, :], reads=[gB])

        def ya_hook(stores):
            for blk in range(0, 8, 2):
                cx.allgather(G2src[blk * 128:(blk + 1) * 128, :], G2[blk * 512:(blk + 1) * 512, :], after=stores)

        cx.bind = {"M1": M1, "G2src": G2src, "m1A": m1A, "m1B": m1B, "m1b_hook": m1b_hook, "e1g": e1g, "ya_hook": ya_hook}
        cx.suffix = f"_p2_{l}"
        with cx.phase(f"L{l}p2_"):
            halo_fix(cx, E1, 4, F32, pTh, mL, mLb, mR, mRb, e_reads=[e1g])
            build_p2(cx, lam_init)
        if stop == 5:
            break
        P.barrier()
        for blk in range(1, 8, 2):
            cx.allgather(G2src[blk * 128:(blk + 1) * 128, :], G2[blk * 512:(blk + 1) * 512, :])
        P.barrier()
        if stop == 6:
            break
        dumb = Buf()
        cx.load_dyn(M2.rearrange("(a r) c -> a r c", a=1), dumb,
                    lambda e: G2.rearrange("(t r) c -> t r c", t=4)[bass.ds(cx.rank(e), 1), :, :])
        P.barrier()
        cx.suffix = f"_p3a_{l}"
        cx.bind = {"x": xin, "xT": xTs, "M2": M2, "pTh": pTh, "gbT": gbTs, "x1": x1s, "x1T": x1Ts}
        with cx.phase(f"L{l}p3a_"):
            build_p3a(cx)
        if stop == 7:
            break
        cx.suffix = f"_p3b_{l}"
        cx.bind = {"x1": x1s, "x1T": x1Ts, "x2": x2s, "x2T": x2Ts, "E3src": E3src}
        with cx.phase(f"L{l}p3b_"):
            build_p3b(cx)
        if stop == 8:
            break
        P.barrier()
        cx.allgather(E3src, E3)
        P.barrier()
        if stop == 9:
            break
        cx.suffix = f"_p4_{l}"
        cx.bind = {"x2": x2s, "x2T": x2Ts, "E3": E3, "masks": (mL, mLb, mR, mRb), "x3": x3s if l < DEPTH - 1 else out_d}
        with cx.phase(f"L{l}p4_"):
            build_p4(cx)
        if stop == 10:
            break
    if stop < 99:
        P.barrier()
        for nm, ap in (("xres", xres), ("M1", M1), ("pThs", pTh), ("G2src", G2src), ("M2", M2), ("x1s", x1s), ("x2s", x2s), ("x3s", x3s), ("E1", E1), ("E3", E3), ("E3src", E3src)):
            shp = list(ap.shape)
            dd = nc.dram_tensor("dbg_" + nm, shp, ap.tensor.dtype, kind="ExternalOutput").ap()
            bb = Buf()
            P.add("sp", lambda e, dd=dd, ap=ap: e.dma_start(out=dd, in_=ap), writes=[bb], dma=True)
            cx.outs.append(bb)
    cx.done()
    return nc


def _bc(v):
    return np.ascontiguousarray(np.broadcast_to(np.asarray(v, np.float32)[None, :], (128, D)))


def host_inputs(inp):
    common = {}
    per_core = [dict() for _ in range(8)]
    for l in range(DEPTH):
        w_in = np.asarray(inp["w_in"][l])
        common[f"w_p1_{l}"] = p1_weight(w_in)
        if l == 0:
            common["g_p1_0"] = _bc(inp["emb_ln_g"])
            common["b_p1_0"] = _bc(inp["emb_ln_b"])
        lamv = np.stack([np.asarray(inp[k][l]) for k in ("lam_q1", "lam_k1", "lam_q2", "lam_k2")], 0).reshape(1, -1)
        common[f"lamv_p2_{l}"] = np.ascontiguousarray(np.broadcast_to(lamv, (128, 256))).astype(np.float32)
        common[f"gs_p2_{l}"] = np.ascontiguousarray(np.asarray(inp["subln_g"][l]).reshape(128, 1)).astype(np.float32)
        common[f"wg_p3a_{l}"] = np.ascontiguousarray(w_in[:, 4608:7680])
        common[f"wb_p3a_{l}"] = np.ascontiguousarray(np.asarray(inp["w_branch"][l]))
        common[f"wm_p3a_{l}"] = np.ascontiguousarray(np.asarray(inp["w_mix_out"][l]))
        cwl = np.asarray(inp["sc_conv_w"][l])
        common[f"cw_p3a_{l}"] = np.ascontiguousarray(cwl.reshape(3, 4, 128).transpose(2, 1, 0).reshape(128, 12))
        common[f"g_p3a_{l}"] = _bc(inp["ln_g"][l, 0])
        common[f"b_p3a_{l}"] = _bc(inp["ln_b"][l, 0])
        common[f"wq_p3b_{l}"] = np.ascontiguousarray(np.asarray(inp["xa_q"][l]))
        common[f"wkv_p3b_{l}"] = np.ascontiguousarray(np.asarray(inp["xa_kv"][l]))
        common[f"wo_p3b_{l}"] = np.ascontiguousarray(np.asarray(inp["xa_o"][l]))
        common[f"g_p3b_{l}"] = _bc(inp["ln_g"][l, 1])
        common[f"b_p3b_{l}"] = _bc(inp["ln_b"][l, 1])
        common[f"wi_p4_{l}"] = np.ascontiguousarray(np.asarray(inp["ffn_w_in"][l]))
        common[f"wo_p4_{l}"] = np.ascontiguousarray(np.asarray(inp["ffn_w_out"][l]))
        cwf = np.concatenate([np.asarray(inp["ffn_conv_w"][l]), np.asarray(inp["ffn_conv_b"][l])[None, :]], axis=0)
        common[f"cwf_p4_{l}"] = np.ascontiguousarray(cwf.reshape(4, NF, 128).transpose(2, 1, 0).reshape(128, NF * 4))
        common[f"g_p4_{l}"] = _bc(inp["ln_g"][l, 2])
        common[f"b_p4_{l}"] = _bc(inp["ln_b"][l, 2])
        rpb_l = np.asarray(inp["rpb"][l])
        for c in range(8):
            j = c % 4
            per_core[c][f"tab_p2_{l}"] = na_table(rpb_l, (2 * j, 2 * j + 1))
            cosT, sinT = rope_tables(j)
            per_core[c][f"cosT_p1_{l}"] = cosT
            per_core[c][f"sinT_p1_{l}"] = sinT
    maps = []
    for c in range(8):
        b, tb = c // 4, c % 4
        m = dict(common)
        m.update(per_core[c])
        m["x_p1_0"] = np.ascontiguousarray(np.asarray(inp["x"])[b, tb * T:(tb + 1) * T])
        for l in range(DEPTH):
            m[f"mem_p3b_{l}"] = np.ascontiguousarray(np.asarray(inp["mem"])[b])
        mL = np.zeros((128, 4), np.float32)
        mR = np.zeros((128, 4), np.float32)
        if tb > 0:
            mL[:, tb - 1] = 1.0
        if tb < 3:
            mR[:, tb + 1] = 1.0
        m["mL"] = mL
        m["mR"] = mR
        maps.append(m)
    return maps


_NC = []


def kernel(**inp):
    inp = {k: np.asarray(v) for k, v in inp.items()}
    if not _NC:
        _NC.append(build_fused())
    res = run_bass_kernel_spmd(_NC[0], host_inputs(inp), core_ids=list(range(8))).results
    out = np.stack([np.concatenate([res[b * 4 + t]["out"] for t in range(4)], axis=0) for b in range(2)], 0)
    return np.ascontiguousarray(out.astype(np.float32))
```
